# Optimizing a Trainium2 kernel written in Bass

```python
import jax, jax.numpy as jnp
from jax import lax
import numpy as np

D_MODEL = 1024
BATCH = 8
SEQ = 2048
DEPTH = 4

HEAD_DIM = 64
A_HEADS = 8
A_WIDTH = A_HEADS * HEAD_DIM
B_WIDTH = 512
C_GROUPS = 4
C_WIDTH = 512
D_HEADS = 8
D_WIDTH = D_HEADS * HEAD_DIM
MOBA_BLOCK = 256
MOBA_TOPK = 3
MOBA_Q_CHUNK = 128
CONV_K = 31
SGU_CHUNK = 128
SB_Q_BLOCK = 128
D_FF = 2816
ROPE_THETA = 10000.0
RMS_EPS = 1e-6
LN_EPS = 1e-5
N_EVEN = (DEPTH + 1) // 2
N_ODD = DEPTH // 2
AB_IN = 3 * A_WIDTH + 2 * B_WIDTH
CD_IN = 2 * C_WIDTH + 3 * D_WIDTH

kernel_name = "hybrid_moba_conv_gmlp_stickbreak_trunk"


def rmsnorm(x, g):
    xf = x.astype(jnp.float32)
    y = xf * lax.rsqrt(jnp.mean(xf * xf, axis=-1, keepdims=True) + RMS_EPS)
    return (y * g.astype(jnp.float32)).astype(x.dtype)


def layernorm(x, g, b):
    xf = x.astype(jnp.float32)
    mu = jnp.mean(xf, axis=-1, keepdims=True)
    var = jnp.mean(jnp.square(xf - mu), axis=-1, keepdims=True)
    y = (xf - mu) * lax.rsqrt(var + LN_EPS)
    return (y * g.astype(jnp.float32) + b.astype(jnp.float32)).astype(x.dtype)


def swiglu(h, w_gate, w_up, w_down):
    return (jax.nn.silu(h @ w_gate) * (h @ w_up)) @ w_down


def rope_tables(seq):
    pos = jnp.arange(seq, dtype=jnp.float32)
    inv = ROPE_THETA ** (-jnp.arange(0, HEAD_DIM, 2, dtype=jnp.float32) / HEAD_DIM)
    ang = pos[:, None] * inv[None, :]
    return jnp.cos(ang), jnp.sin(ang)


def apply_rope(x, cos, sin):
    half = HEAD_DIM // 2
    xf = x.astype(jnp.float32)
    x1, x2 = xf[..., :half], xf[..., half:]
    c, s = cos[None, :, None, :], sin[None, :, None, :]
    return jnp.concatenate([x1 * c - x2 * s, x2 * c + x1 * s], axis=-1).astype(x.dtype)


def moba_attention(q, k, v):
    bsz, seq, nh, dh = q.shape
    n_blk = -(-seq // MOBA_BLOCK)
    pad = n_blk * MOBA_BLOCK - seq
    kp = jnp.pad(k, ((0, 0), (0, pad), (0, 0), (0, 0)))
    vp = jnp.pad(v, ((0, 0), (0, pad), (0, 0), (0, 0)))
    k_blocks = kp.reshape(bsz, n_blk, MOBA_BLOCK, nh, dh).transpose(0, 3, 1, 2, 4)
    v_blocks = vp.reshape(bsz, n_blk, MOBA_BLOCK, nh, dh).transpose(0, 3, 1, 2, 4)
    k_mean = jnp.mean(k_blocks.astype(jnp.float32), axis=3).astype(k.dtype)
    qh = q.transpose(0, 2, 1, 3)
    n_qc = seq // MOBA_Q_CHUNK
    top = min(MOBA_TOPK, n_blk)
    scale = HEAD_DIM ** -0.5
    blk_ids = jnp.arange(n_blk)
    gather_blocks = jax.vmap(lambda kb_h, idx_h: kb_h[idx_h])

    def one_chunk(bc):
        b, c = bc
        start = c * MOBA_Q_CHUNK
        q_c = lax.dynamic_slice_in_dim(qh[b], start, MOBA_Q_CHUNK, axis=1)
        kb, vb = k_blocks[b], v_blocks[b]
        own = start // MOBA_BLOCK
        q_pos = start + jnp.arange(MOBA_Q_CHUNK)
        gate = jnp.einsum('hqd,hnd->hqn', q_c, k_mean[b]).astype(jnp.float32)
        gate = jnp.where(blk_ids[None, None, :] < own, gate, -jnp.inf)
        _, idx = lax.top_k(gate, top)
        valid = idx < own
        k_sel = gather_blocks(kb, idx)
        v_sel = gather_blocks(vb, idx)
        s_sel = jnp.einsum('hqd,hqjpd->hqjp', q_c, k_sel).astype(jnp.float32) * scale
        s_sel = jnp.where(valid[..., None], s_sel, -jnp.inf).reshape(nh, MOBA_Q_CHUNK, top * MOBA_BLOCK)
        k_own = lax.dynamic_index_in_dim(kb, own, axis=1, keepdims=False)
        v_own = lax.dynamic_index_in_dim(vb, own, axis=1, keepdims=False)
        s_own = jnp.einsum('hqd,hpd->hqp', q_c, k_own).astype(jnp.float32) * scale
        key_pos = own * MOBA_BLOCK + jnp.arange(MOBA_BLOCK)
        s_own = jnp.where(key_pos[None, None, :] <= q_pos[None, :, None], s_own, -jnp.inf)
        p = jax.nn.softmax(jnp.concatenate([s_sel, s_own], axis=-1), axis=-1).astype(v.dtype)
        p_sel = p[..., :top * MOBA_BLOCK].reshape(nh, MOBA_Q_CHUNK, top, MOBA_BLOCK)
        p_own = p[..., top * MOBA_BLOCK:]
        return (jnp.einsum('hqjp,hqjpd->hqd', p_sel, v_sel)
                + jnp.einsum('hqp,hpd->hqd', p_own, v_own))

    b_ids = jnp.repeat(jnp.arange(bsz), n_qc)
    c_ids = jnp.tile(jnp.arange(n_qc), bsz)
    o = lax.map(one_chunk, (b_ids, c_ids))
    o = o.reshape(bsz, n_qc, nh, MOBA_Q_CHUNK, dh).transpose(0, 1, 3, 2, 4)
    return o.reshape(bsz, seq, nh * dh)


def conformer_conv(a, g, conv_w, conv_b, ln_g, ln_b):
    h = a * jax.nn.sigmoid(g)
    h = lax.conv_general_dilated(
        h, conv_w[:, None, :].astype(h.dtype), window_strides=(1,),
        padding=[(CONV_K - 1, 0)], dimension_numbers=('NWC', 'WIO', 'NWC'),
        feature_group_count=B_WIDTH) + conv_b
    return jax.nn.silu(layernorm(h, ln_g, ln_b))


def chunked_sgu(u, v, ln_g, ln_b, w_s, b_s):
    u = jax.nn.gelu(u)
    v = layernorm(jax.nn.gelu(v), ln_g, ln_b)
    bsz, seq, ch = v.shape
    vg = v.reshape(bsz, seq // SGU_CHUNK, SGU_CHUNK, C_GROUPS, ch // C_GROUPS)
    causal = jnp.tril(jnp.ones((SGU_CHUNK, SGU_CHUNK), dtype=w_s.dtype))
    mixed = jnp.einsum('gts,bnsgc->bntgc', w_s * causal, vg) + b_s.T[None, None, :, :, None]
    return u * mixed.reshape(bsz, seq, ch)


def stick_breaking_attention(q, k, v):
    bsz, seq, nh, dh = q.shape
    qh, kh, vh = (t.transpose(0, 2, 1, 3) for t in (q, k, v))
    scale = HEAD_DIM ** -0.5
    s_pos = jnp.arange(seq)
    n_qb = seq // SB_Q_BLOCK

    def one_block(i):
        start = i * SB_Q_BLOCK
        qb = lax.dynamic_slice_in_dim(qh, start, SB_Q_BLOCK, axis=2)
        z = jnp.einsum('bhqd,bhkd->bhqk', qb, kh).astype(jnp.float32) * scale
        t_pos = start + jnp.arange(SB_Q_BLOCK)
        strict = s_pos[None, :] < t_pos[:, None]
        log_1m = jnp.where(strict, jax.nn.log_sigmoid(-z), 0.0)
        after = lax.cumsum(log_1m, axis=3, reverse=True) - log_1m
        log_a = jnp.where(strict, jax.nn.log_sigmoid(z) + after, -jnp.inf)
        att = jnp.exp(log_a).astype(v.dtype)
        return jnp.einsum('bhqk,bhkd->bhqd', att, vh)

    o = lax.map(one_block, jnp.arange(n_qb))
    return o.transpose(1, 0, 3, 2, 4).reshape(bsz, seq, nh * dh)


def even_mixer(h, w_in, w_out, conv_w, conv_b, ln_g, ln_b, cos, sin):
    bsz, seq, _ = h.shape
    proj = h @ w_in
    q, k, v, ga, gb = jnp.split(proj, [A_WIDTH, 2 * A_WIDTH, 3 * A_WIDTH, 3 * A_WIDTH + B_WIDTH], axis=-1)
    q = apply_rope(q.reshape(bsz, seq, A_HEADS, HEAD_DIM), cos, sin)
    k = apply_rope(k.reshape(bsz, seq, A_HEADS, HEAD_DIM), cos, sin)
    v = v.reshape(bsz, seq, A_HEADS, HEAD_DIM)
    o_a = moba_attention(q, k, v)
    o_b = conformer_conv(ga, gb, conv_w, conv_b, ln_g, ln_b)
    return jnp.concatenate([o_a, o_b], axis=-1) @ w_out


def odd_mixer(h, w_in, w_out, ln_g, ln_b, w_s, b_s):
    bsz, seq, _ = h.shape
    proj = h @ w_in
    u, vc, q, k, v = jnp.split(proj, [C_WIDTH, 2 * C_WIDTH, 2 * C_WIDTH + D_WIDTH, 2 * C_WIDTH + 2 * D_WIDTH], axis=-1)
    o_c = chunked_sgu(u, vc, ln_g, ln_b, w_s, b_s)
    o_d = stick_breaking_attention(q.reshape(bsz, seq, D_HEADS, HEAD_DIM),
                                   k.reshape(bsz, seq, D_HEADS, HEAD_DIM),
                                   v.reshape(bsz, seq, D_HEADS, HEAD_DIM))
    return jnp.concatenate([o_c, o_d], axis=-1) @ w_out


def setup_inputs(seed: int = 0) -> dict:
    key = jax.random.key(seed)
    ks = jax.random.split(key, 17)
    f32 = jnp.float32
    nrm = lambda k, shape, s: jax.random.normal(k, shape, f32) * s
    return {
        'x': jax.random.normal(ks[0], (BATCH, SEQ, D_MODEL), f32),
        'norm_g': 1.0 + nrm(ks[1], (DEPTH, 6, D_MODEL), 0.02),
        'ffn_w_gate': nrm(ks[2], (DEPTH, 2, D_MODEL, D_FF), D_MODEL ** -0.5),
        'ffn_w_up': nrm(ks[3], (DEPTH, 2, D_MODEL, D_FF), D_MODEL ** -0.5),
        'ffn_w_down': nrm(ks[4], (DEPTH, 2, D_FF, D_MODEL), D_FF ** -0.5),
        'ab_w_in': nrm(ks[5], (N_EVEN, D_MODEL, AB_IN), D_MODEL ** -0.5),
        'ab_w_out': nrm(ks[6], (N_EVEN, A_WIDTH + B_WIDTH, D_MODEL), (A_WIDTH + B_WIDTH) ** -0.5),
        'conv_w': nrm(ks[7], (N_EVEN, CONV_K, B_WIDTH), CONV_K ** -0.5),
        'conv_b': nrm(ks[8], (N_EVEN, B_WIDTH), 0.02),
        'conv_ln_g': 1.0 + nrm(ks[9], (N_EVEN, B_WIDTH), 0.02),
        'conv_ln_b': nrm(ks[10], (N_EVEN, B_WIDTH), 0.02),
        'cd_w_in': nrm(ks[11], (N_ODD, D_MODEL, CD_IN), D_MODEL ** -0.5),
        'cd_w_out': nrm(ks[12], (N_ODD, C_WIDTH + D_WIDTH, D_MODEL), (C_WIDTH + D_WIDTH) ** -0.5),
        'sgu_ln_g': 1.0 + nrm(ks[13], (N_ODD, C_WIDTH), 0.02),
        'sgu_ln_b': nrm(ks[14], (N_ODD, C_WIDTH), 0.02),
        'sgu_w': nrm(ks[15], (N_ODD, C_GROUPS, SGU_CHUNK, SGU_CHUNK), SGU_CHUNK ** -0.5),
        'sgu_b': 1.0 + nrm(ks[16], (N_ODD, C_GROUPS, SGU_CHUNK), 0.02),
    }


def reference(x, norm_g, ffn_w_gate, ffn_w_up, ffn_w_down, ab_w_in, ab_w_out,
              conv_w, conv_b, conv_ln_g, conv_ln_b, cd_w_in, cd_w_out,
              sgu_ln_g, sgu_ln_b, sgu_w, sgu_b):
    seq = x.shape[1]
    cos, sin = rope_tables(seq)
    for layer in range(DEPTH):
        g = norm_g[layer]
        f = swiglu(rmsnorm(x, g[0]), ffn_w_gate[layer, 0], ffn_w_up[layer, 0], ffn_w_down[layer, 0])
        x = x + 0.5 * rmsnorm(f, g[1])
        h = rmsnorm(x, g[2])
        i = layer // 2
        if layer % 2 == 0:
            m = even_mixer(h, ab_w_in[i], ab_w_out[i], conv_w[i], conv_b[i],
                           conv_ln_g[i], conv_ln_b[i], cos, sin)
        else:
            m = odd_mixer(h, cd_w_in[i], cd_w_out[i], sgu_ln_g[i], sgu_ln_b[i], sgu_w[i], sgu_b[i])
        x = x + rmsnorm(m, g[3])
        f = swiglu(rmsnorm(x, g[4]), ffn_w_gate[layer, 1], ffn_w_up[layer, 1], ffn_w_down[layer, 1])
        x = x + 0.5 * rmsnorm(f, g[5])
    return x
```

```python
import numpy as np
from contextlib import ExitStack
import concourse.bass as bass
import concourse.mybir as mybir
from concourse.bass_utils import run_bass_kernel_spmd

F32 = mybir.dt.float32
BF16 = mybir.dt.bfloat16
AF = mybir.ActivationFunctionType
ALU = mybir.AluOpType
AX = mybir.AxisListType

S = 2048
D = 1024
DFF = 2816
NT = 16
KC = 8
NJ = 22
DEPTH = 4
RMS_EPS = 1e-6
LN_EPS = 1e-5
NSLOT = 8
NHOLD = 4
SLOT_ELEMS = 2048
ARENA_ELEMS = 50 * 1024
NEG_BIG = -30000.0


class Buf:
    __slots__ = ("name", "w", "rs")

    def __init__(self, name):
        self.name = name
        self.w = None
        self.rs = {}


class Sem:
    __slots__ = ("h", "name", "cnt")

    def __init__(self, h, name):
        self.h = h
        self.name = name
        self.cnt = 0


class KB:
    def __init__(self, nc, es):
        self.nc = nc
        self.es = es
        self.E = {"pe": nc.tensor, "act": nc.scalar, "dve": nc.vector, "pool": nc.gpsimd, "sp": nc.sync}
        self.sem = {e: self.new_sem("eng_" + e) for e in self.E}
        self.seen = {e: {} for e in self.E}
        self.nwait = 0
        self.nins = 0

    def new_sem(self, name):
        return Sem(self.es.enter_context(self.nc.semaphore(name)), name)

    def _deps(self, reads, writes):
        d = {}

        def add(t):
            if t is None:
                return
            s, v = t
            if s.name not in d or d[s.name][1] < v:
                d[s.name] = (s, v)

        for b in reads:
            add(b.w)
        for b in writes:
            add(b.w)
            for t in b.rs.values():
                add(t)
        return d

    def _wait(self, e, deps):
        own = self.sem[e]
        for name, (sem, val) in deps.items():
            if sem is own and (e == "pe" or val > sem.cnt):
                continue
            if self.seen[e].get(name, 0) >= val:
                continue
            self.E[e].wait_ge(sem.h, val)
            self.seen[e][name] = val
            self.nwait += 1

    def op(self, e, fn, reads=(), writes=(), inc=True):
        self._wait(e, self._deps(reads, writes))
        ins = fn(self.E[e])
        sem = self.sem[e]
        val = sem.cnt + 1
        if inc:
            ins.then_inc(sem.h, 1)
            sem.cnt = val
        t = (sem, val)
        for b in reads:
            b.rs[sem.name] = t
        for b in writes:
            b.w = t
            b.rs = {}
        self.nins += 1
        return ins

    def dma(self, q, out, in_, sem, reads=(), writes=()):
        self._wait(q, self._deps(reads, writes))
        ins = self.E[q].dma_start(out=out, in_=in_)
        ins.then_inc(sem.h, 16)
        sem.cnt += 16
        t = (sem, sem.cnt)
        for b in reads:
            b.rs[sem.name] = t
        for b in writes:
            b.w = t
            b.rs = {}
        self.nins += 1
        return ins

    def barrier(self, engines=("pe", "act", "dve", "sp")):
        for e in engines:
            for o in engines:
                if o == e or o == "sp":
                    continue
                s = self.sem[o]
                if s.cnt > self.seen[e].get(s.name, 0):
                    self.E[e].wait_ge(s.h, s.cnt)
                    self.seen[e][s.name] = s.cnt


class WStream:
    def __init__(self, kb, slots, jobs):
        self.kb = kb
        self.slots = slots
        self.jobs = jobs
        self.issued = 0
        self.next = 0

    def _issue(self, k):
        shape, src = self.jobs[k]
        th, buf, sem = self.slots[k % NSLOT]
        n = 1
        for v in shape[1:]:
            n *= v
        dst = th[:, 0:n]
        if len(shape) == 3:
            dst = dst.rearrange("p (a b) -> p a b", a=shape[1])
        self.kb.dma("pool", dst, src, sem, writes=[buf])

    def acquire(self, shape, hold=NHOLD):
        k = self.next
        assert k < len(self.jobs), "weight job list exhausted"
        assert tuple(self.jobs[k][0]) == tuple(shape), (k, self.jobs[k][0], shape)
        while self.issued < min(len(self.jobs), k + NSLOT - hold + 1):
            self._issue(self.issued)
            self.issued += 1
        self.next += 1
        th, buf, sem = self.slots[k % NSLOT]
        n = 1
        for v in shape[1:]:
            n *= v
        v = th[:, 0:n]
        if len(shape) == 3:
            v = v.rearrange("p (a b) -> p a b", a=shape[1])
        return v, buf


class Prog:
    def __init__(self, layers, stages=None):
        self.layers = list(layers)
        self.stages = stages
        self.nc = bass.Bass("TRN2", target_bir_lowering=False)
        self.build()

    def dram_in(self, name, shape, dt=F32):
        return self.nc.dram_tensor(name, list(shape), dt, kind="ExternalInput").ap()

    def declare(self):
        self.x_d = self.dram_in("x", [S, D])
        self.out_d = self.nc.dram_tensor("out", [S, D], F32, kind="ExternalOutput").ap()
        self.gfm_d = self.dram_in("g_fm", [128, DEPTH * 6 * KC])
        self.gtm_d = self.dram_in("norm_g", [DEPTH, 6, D])
        self.wg_d = self.dram_in("ffn_w_gate", [DEPTH, 2, D, DFF])
        self.wu_d = self.dram_in("ffn_w_up", [DEPTH, 2, D, DFF])
        self.wd_d = self.dram_in("ffn_w_down", [DEPTH, 2, DFF, D])
        self.cb_d = self.dram_in("consts_bf", [128, CB_COLS])
        self.abin_d = self.dram_in("ab_w_in_ext", [2, D, 3584])
        self.about_d = self.dram_in("ab_w_out", [2, D, D])
        self.cdin_d = self.dram_in("cd_w_in", [2, D, 2560])
        self.cdout_d = self.dram_in("cd_w_out", [2, D, D])
        self.convfm_d = self.dram_in("conv_fm", [128, 2 * CONV_FM])
        self.rope_d = self.dram_in("rope_cs", [2, 128, S])
        self.sguw_d = self.dram_in("sgu_wT", [2, 128, 4 * 128])
        self.sgub_d = self.dram_in("sgu_b", [2, 512])
        self.sguln_d = self.dram_in("sgu_ln", [4, 512])

    def want(self, stage):
        return self.stages is None or stage in self.stages

    def ffn_jobs(self, l, i):
        jobs = []
        wg = self.wg_d[l, i].rearrange("(kc p) f -> p kc f", p=128)
        wu = self.wu_d[l, i].rearrange("(kc p) f -> p kc f", p=128)
        wd = self.wd_d[l, i].rearrange("(j p) d -> p j d", p=128)
        for G in range(2):
            for s in range(NJ // 2):
                jobs.append(((128, KC, 256), wg[:, :, s * 256:(s + 1) * 256]))
                jobs.append(((128, KC, 256), wu[:, :, s * 256:(s + 1) * 256]))
            for c in range(2):
                for sd in range(6):
                    nj = 4 if sd < 5 else 2
                    jobs.append(((128, nj, 512), wd[:, 4 * sd:4 * sd + nj, c * 512:(c + 1) * 512]))
        return jobs

    def av(self, off, shape, dt):
        n = 1
        for v in shape[1:]:
            n *= v
        units = n * (2 if dt == F32 else 1)
        assert off + units <= ARENA_ELEMS, (off, units)
        v = self.arena[:, off:off + units]
        if dt == F32:
            v = v.bitcast(F32)
        if len(shape) == 3:
            v = v.rearrange("p (a b) -> p a b", a=shape[1])
        return v

    def bank(self, b):
        return self.ps[:, b, :]

    def dbank(self, p):
        return self.ps[:, 2 * p:2 * p + 2, :]

    def build(self):
        nc = self.nc
        self.declare()
        with ExitStack() as es:
            kb = self.kb = KB(nc, es)
            self.xs = es.enter_context(nc.sbuf_tensor("xs", [128, NT, D], F32))
            self.arena = es.enter_context(nc.sbuf_tensor("arena", [128, ARENA_ELEMS], BF16))
            self.ps = es.enter_context(nc.psum_tensor("ps", [128, 8, 512], F32))
            self.cb = es.enter_context(nc.sbuf_tensor("cb", [128, CB_COLS], BF16))
            self.gfm = es.enter_context(nc.sbuf_tensor("gfm", [128, DEPTH * 6 * KC], F32))
            self.gbc = es.enter_context(nc.sbuf_tensor("gbc", [128, D], F32))
            self.stat = es.enter_context(nc.sbuf_tensor("stat", [128, 64], F32))
            self.cf = es.enter_context(nc.sbuf_tensor("cf", [128, 16], F32))
            self._eps = {}
            for v in (RMS_EPS, 4.0 * RMS_EPS, LN_EPS, 1.0):
                self.eps_ap(v)
            nc.vector.memset(self.stat[:, :], 0.0).then_inc(kb.sem["dve"].h, 1)
            kb.sem["dve"].cnt += 1
            self.b_cf = Buf("cf")
            self.b_cf.w = (kb.sem["dve"], 1)
            slots = []
            for k in range(NSLOT):
                th = es.enter_context(nc.sbuf_tensor(f"wslot{k}", [128, SLOT_ELEMS], BF16))
                slots.append((th, Buf(f"wslot{k}"), kb.new_sem(f"wsem{k}")))
            self.xb = [Buf(f"x{t}") for t in range(NT)]
            self.pb = [Buf(f"ps{b}") for b in range(8)]
            self.b_gbc = Buf("gbc")
            self.b_stat = Buf("stat")
            self.b_stat.w = (kb.sem["dve"], 1)
            self.b_const = Buf("const")
            self.sem_in = kb.new_sem("sem_in")
            self.sem_out = kb.new_sem("sem_out")
            self.sem_g = kb.new_sem("sem_g")

            jobs = []
            for l in self.layers:
                if self.want("ffn0"):
                    jobs += self.ffn_jobs(l, 0)
                if self.want("mix"):
                    jobs += self.mixer_jobs(l)
                if self.want("ffn1"):
                    jobs += self.ffn_jobs(l, 1)
            self.ws = WStream(kb, slots, jobs)

            self.sem_cb = kb.new_sem("sem_cb")
            kb.dma("pool", self.cb[:, :], self.cb_d[:, :], self.sem_cb, writes=[self.b_const])
            tot_cb = (self.sem_cb, self.sem_cb.cnt)
            kb.dma("sp", self.gfm[:, :], self.gfm_d[:, :], self.sem_in)
            for t in range(NT):
                kb.dma("sp", self.xs[:, t, :], self.x_d[t * 128:(t + 1) * 128, :], self.sem_in,
                       writes=[self.xb[t]])
            tot = (self.sem_in, self.sem_in.cnt)
            self.b_const.w = tot_cb
            self.b_gfm = Buf("gfm")
            self.b_gfm.w = tot
            for t in range(NT):
                self.xb[t].w = tot

            self.small = es.enter_context(nc.sbuf_tensor("small", [128, 512], F32))
            self.b_small = Buf("small")
            self.held = set()
            for l in self.layers:
                if self.want("ffn0"):
                    self.ffn(l, 0)
                if self.want("mix"):
                    if l % 2 == 0:
                        self.even_mixer(l)
                    else:
                        self.odd_mixer(l)
                if self.want("ffn1"):
                    self.ffn(l, 1)

            for t in range(NT):
                kb.dma("sp", self.out_d[t * 128:(t + 1) * 128, :], self.xs[:, t, :], self.sem_out,
                       reads=[self.xb[t]])
            nc.sync.wait_ge(self.sem_out.h, self.sem_out.cnt)
            print(f"[build] instructions={kb.nins} waits={kb.nwait}")

    def load_gbc(self, l, i):
        src = self.gtm_d[l, i:i + 1, :].to_broadcast([128, D])
        self.kb.dma("sp", self.gbc[:, :], src, self.sem_g, writes=[self.b_gbc])

    def prenorm_tile(self, l, i, tile, xn, b_xn, junk, b_junk, dst, b_dst, part=3):
        kb = self.kb
        if part == 2:
            return self.prenorm_p2(l, i, xn, b_xn, dst, b_dst)
        xt = self.xs[:, tile, :]
        ss = self.stat[:, 0:1]
        rs = self.stat[:, 1:2]
        kb.op("act", lambda e: e.activation(out=junk, in_=xt, func=AF.Square, accum_out=ss),
              reads=[self.xb[tile]], writes=[b_junk, self.b_stat])
        self.rsqrt(rs, ss, 1.0 / D, RMS_EPS)
        kb.op("dve", lambda e: e.tensor_scalar(out=xn, in0=xt, scalar1=rs, scalar2=None, op0=ALU.mult),
              reads=[self.xb[tile], self.b_stat], writes=[b_xn])
        if part == 1:
            return
        self.prenorm_p2(l, i, xn, b_xn, dst, b_dst)

    def prenorm_p2(self, l, i, xn, b_xn, dst, b_dst, bank=None):
        kb = self.kb
        bk = self.next_bank() if bank is None else bank
        pst = self.bank(bk).bitcast(BF16)
        ident = self.cb[:, C_IDENT:C_IDENT + 128]
        for kc in range(KC):
            kb.op("pe", lambda e, kc=kc: e.transpose(out=pst[:, kc * 128:(kc + 1) * 128],
                                                     in_=xn[:, kc * 128:(kc + 1) * 128], identity=ident),
                  reads=[b_xn, self.b_const], writes=[self.pb[bk]], inc=(kc == KC - 1))
        g = self.gfm[:, (l * 6 + i) * KC:(l * 6 + i + 1) * KC]
        kb.op("dve", lambda e: e.tensor_tensor(out=dst, in0=pst.rearrange("p (a b) -> p a b", a=KC),
                                               in1=g.unsqueeze(2).to_broadcast([128, KC, 128]), op=ALU.mult),
              reads=[self.pb[bk], self.b_gfm], writes=[b_dst])

    def rsqrt(self, out, in_, scale, bias):
        kb = self.kb
        kb.op("act", lambda e: e.activation(out=out, in_=in_, func=AF.Sqrt, bias=self.eps_ap(bias), scale=scale),
              reads=[self.b_stat, self.b_cf], writes=[self.b_stat])
        kb.op("dve", lambda e: e.reciprocal(out=out, in_=out), reads=[self.b_stat], writes=[self.b_stat])

    def eps_ap(self, val):
        if val not in self._eps:
            col = self.cf[:, len(self._eps):len(self._eps) + 1]
            self.nc.vector.memset(col, float(val))
            self._eps[val] = col
        return self._eps[val]

    def next_bank(self, hold=False):
        b = getattr(self, "_bank_rr", -1)
        for _ in range(8):
            b = (b + 1) % 8
            if b not in self.held:
                break
        self._bank_rr = b
        if hold:
            self.held.add(b)
        return b

    def ffn(self, l, i):
        kb = self.kb
        kb.barrier()
        dbg = 99
        if dbg < 1:
            return
        A = 0
        hT0 = self.av(A, [128, KC, 1024], BF16); A += KC * 1024
        hT1 = self.av(36864, [128, KC, 1024], BF16)
        hTs = [hT0, hT1]
        stashes = [self.av(0, [128, 8, 512], F32), self.av(36864, [128, 8, 512], F32)]
        xn1 = self.av(36864 + KC * 1024, [128, D], BF16); b_xn1 = Buf("xn1")
        hid = self.av(A, [128, NJ, 1024], BF16); A += NJ * 1024
        xn = self.av(A, [128, D], BF16); A += D
        junk = self.av(A, [128, D], BF16); A += D
        sg = self.av(A, [128, 1024], F32); A += 2048
        tmp = self.av(A, [128, 1024], F32); A += 2048
        b_hTs, b_xn, b_junk, b_sg, b_tmp = [Buf("hT0"), Buf("hT1")], Buf("xn"), Buf("junk"), Buf("sg"), Buf("tmp")
        b_hid = [Buf(f"hid{j}") for j in range(NJ)]
        ssf = self.stat[:, 8:8 + 16].rearrange("p (t c) -> p t c", c=2)
        b_ssf = Buf("ssf")
        self.load_gbc(l, 1 if i == 0 else 5)
        gi_pre = 0 if i == 0 else 4
        if dbg < 2:
            return
        deferred = []
        for G in range(2):
            hT, b_hT, stash = hTs[G], b_hTs[G], stashes[G]
            if G == 0:
                for t in range(8):
                    self.prenorm_tile(l, gi_pre, t, xn, b_xn, junk, b_junk, hT[:, :, t * 128:(t + 1) * 128], b_hT)
            xnb = [(xn, b_xn), (xn1, b_xn1)]
            if dbg < 3:
                return
            for s in range(NJ // 2):
                wgs, b_wg = self.ws.acquire((128, KC, 256), hold=2)
                wus, b_wu = self.ws.acquire((128, KC, 256), hold=2)
                for jj in range(2):
                    j = 2 * s + jj
                    pg = (2 * j) % 4
                    pu = pg + 1
                    gb = [self.pb[2 * pg], self.pb[2 * pg + 1]]
                    ub = [self.pb[2 * pu], self.pb[2 * pu + 1]]
                    for kc in range(KC):
                        if kc == 4 and G == 0 and j % 2 == 0 and 0 <= (j - 4) // 2 < 8:
                            t_ = (j - 4) // 2
                            xx, b_xx = xnb[t_ % 2]
                            self.prenorm_p2(l, gi_pre, xx, b_xx, hTs[1][:, :, t_ * 128:(t_ + 1) * 128], b_hTs[1], bank=7)
                        for h2 in range(2):
                            kb.op("pe", lambda e, kc=kc, h2=h2: e.matmul(
                                self.ps[:, 2 * pg + h2, :], lhsT=wgs[:, kc, jj * 128:(jj + 1) * 128],
                                rhs=hT[:, kc, h2 * 512:(h2 + 1) * 512], start=(kc == 0), stop=(kc == KC - 1)),
                                reads=[b_wg, b_hT], writes=[gb[h2]], inc=(kc == KC - 1))
                        for h2 in range(2):
                            kb.op("pe", lambda e, kc=kc, h2=h2: e.matmul(
                                self.ps[:, 2 * pu + h2, :], lhsT=wus[:, kc, jj * 128:(jj + 1) * 128],
                                rhs=hT[:, kc, h2 * 512:(h2 + 1) * 512], start=(kc == 0), stop=(kc == KC - 1)),
                                reads=[b_wu, b_hT], writes=[ub[h2]], inc=(kc == KC - 1))
                    if deferred:
                        deferred.pop(0)()
                    kb.op("act", lambda e: e.activation(out=sg.rearrange("p (a b) -> p a b", a=2),
                                                        in_=self.dbank(pg), func=AF.Silu),
                          reads=gb, writes=[b_sg])
                    kb.op("dve", lambda e: e.tensor_tensor(out=hid[:, j, :].rearrange("p (a b) -> p a b", a=2),
                                                           in0=self.dbank(pu),
                                                           in1=sg.rearrange("p (a b) -> p a b", a=2), op=ALU.mult),
                          reads=ub + [b_sg], writes=[b_hid[j]])
                    if G == 0 and j % 2 == 1 and (j - 1) // 2 < 8:
                        t_ = (j - 1) // 2
                        xx, b_xx = xnb[t_ % 2]
                        self.prenorm_tile(l, gi_pre, 8 + t_, xx, b_xx, junk, b_junk,
                                          hTs[1][:, :, t_ * 128:(t_ + 1) * 128], b_hTs[1], part=1)
            if dbg < 4:
                return
            skip = ''
            for c in range(2):
                for sd in range(6):
                    nj = 4 if sd < 5 else 2
                    wds, b_wd = self.ws.acquire((128, nj, 512), hold=2)
                    for jj in range(nj):
                        j = 4 * sd + jj
                        for t in range(8):
                            kb.op("pe", lambda e, t=t, jj=jj, j=j: e.matmul(
                                self.ps[:, t, :], lhsT=hid[:, j, t * 128:(t + 1) * 128], rhs=wds[:, jj, :],
                                start=(j == 0), stop=(j == NJ - 1)),
                                reads=[b_wd, b_hid[j]], writes=[self.pb[t]], inc=(j == NJ - 1 or (jj == nj - 1 and t == 7)))
                for t in range(8):
                    dstv = stash[:, t, :] if c == 0 else hid[:, t, :].bitcast(F32)
                    b_d = b_hT if c == 0 else b_hid[t]
                    kb.op("dve", lambda e, t=t, dstv=dstv: e.tensor_copy(out=dstv, in_=self.ps[:, t, :]),
                          reads=[self.pb[t]], writes=[b_d])
                    kb.op("act", lambda e, t=t, dstv=dstv: e.activation(out=junk[:, 0:512], in_=dstv, func=AF.Square,
                                                                        accum_out=self.stat[:, 8 + 2 * t + c:9 + 2 * t + c]),
                          reads=[b_d], writes=[b_junk, b_ssf])
                if c == 1:
                    for t in range(8):
                        def resid(t=t, tile=G * 8 + t, stash=stash, b_hT=b_hT):
                            rs = self.stat[:, 2:3]
                            kb.op("dve", lambda e: e.tensor_tensor(out=rs, in0=self.stat[:, 8 + 2 * t:9 + 2 * t],
                                                                   in1=self.stat[:, 9 + 2 * t:10 + 2 * t], op=ALU.add),
                                  reads=[b_ssf], writes=[self.b_stat])
                            self.rsqrt(rs, rs, 4.0 / D, 4.0 * RMS_EPS)
                            kb.op("dve", lambda e: e.scalar_tensor_tensor(
                                out=tmp[:, 0:512], in0=stash[:, t, :], scalar=rs, in1=self.gbc[:, 0:512],
                                op0=ALU.mult, op1=ALU.mult),
                                reads=[b_hT, self.b_stat, self.b_gbc], writes=[b_tmp])
                            kb.op("dve", lambda e: e.scalar_tensor_tensor(
                                out=tmp[:, 512:1024], in0=hid[:, t, :].bitcast(F32), scalar=rs, in1=self.gbc[:, 512:1024],
                                op0=ALU.mult, op1=ALU.mult),
                                reads=[b_hid[t], self.b_stat, self.b_gbc], writes=[b_tmp])
                            kb.op("dve", lambda e: e.tensor_tensor(
                                out=self.xs[:, tile, :], in0=self.xs[:, tile, :], in1=tmp, op=ALU.add),
                                reads=[b_tmp, self.xb[tile]], writes=[self.xb[tile]])
                        deferred.append(resid)


        while deferred:
            deferred.pop(0)()

    def mixer_jobs(self, l):
        i = l // 2
        jobs = []
        if l % 2 == 0:
            w = self.abin_d[i].rearrange("(kc p) f -> p kc f", p=128)
            sl = lambda c0: ((128, KC, 256), w[:, :, c0:c0 + 256])
            for grp in range(2):
                for p in range(2):
                    jobs.append(sl(grp * 512 + p * 256))
                    jobs.append(sl(2560 + grp * 512 + p * 256))
            jobs.append(sl(1024)); jobs.append(sl(1280))
            for p in range(2):
                jobs.append(sl(1536 + p * 256))
                jobs.append(sl(2048 + p * 256))
            wo = self.about_d[i].rearrange("(kc p) f -> p kc f", p=128)
        else:
            w = self.cdin_d[i].rearrange("(kc p) f -> p kc f", p=128)
            sl = lambda c0: ((128, KC, 256), w[:, :, c0:c0 + 256])
            for c0 in (0, 256, 512, 768, 1024, 1280, 1536, 1792, 2048, 2304):
                jobs.append(sl(c0))
            wo = self.cdout_d[i].rearrange("(kc p) f -> p kc f", p=128)
        for s4 in range(4):
            jobs.append(((128, KC, 256), wo[:, :, s4 * 256:(s4 + 1) * 256]))
        return jobs

    def mm_fm(self, bk, slab, b_slab, fc, hT, b_hT, tg):
        for kc in range(KC):
            self.kb.op("pe", lambda e, kc=kc: e.matmul(
                self.ps[:, bk, :], lhsT=slab[:, kc, fc * 128:(fc + 1) * 128],
                rhs=hT[:, kc, tg * 512:(tg + 1) * 512], start=(kc == 0), stop=(kc == KC - 1)),
                reads=[b_slab, b_hT], writes=[self.pb[bk]], inc=(kc == KC - 1))

    def mm_tm(self, bk, slabs, b_slabs, hT, b_hT, tile):
        for s2 in range(2):
            for kc in range(KC):
                self.kb.op("pe", lambda e, kc=kc, s2=s2: e.matmul(
                    self.ps[:, bk, s2 * 256:(s2 + 1) * 256], lhsT=hT[:, kc, tile * 128:(tile + 1) * 128],
                    rhs=slabs[s2][:, kc, :], start=(kc == 0), stop=(kc == KC - 1)),
                    reads=[b_slabs[s2], b_hT], writes=[self.pb[bk]], inc=(kc == KC - 1))

    def prenorm_all(self, l, gi, hT, b_hT, xn, b_xn, junk, b_junk):
        for tile in range(NT):
            self.prenorm_tile(l, gi, tile, xn, b_xn, junk, b_junk, hT[:, :, tile * 128:(tile + 1) * 128], b_hT)

    def out_proj(self, l, srcs, b_srcs, tmp, b_tmp, junk, b_junk):
        kb = self.kb
        self.load_gbc(l, 3)
        slabs = [self.ws.acquire((128, KC, 256)) for _ in range(4)]
        for tile in range(NT):
            b0 = self.next_bank(); b1 = self.next_bank()
            bb = [b0, b1]
            for s4 in range(4):
                for ch in range(8):
                    kb.op("pe", lambda e, s4=s4, ch=ch: e.matmul(
                        self.ps[:, bb[s4 // 2], (s4 % 2) * 256:(s4 % 2) * 256 + 256],
                        lhsT=srcs[ch][:, tile * 128:(tile + 1) * 128], rhs=slabs[s4][0][:, ch, :],
                        start=(ch == 0), stop=(ch == 7)),
                        reads=[slabs[s4][1], b_srcs[ch]], writes=[self.pb[bb[s4 // 2]]], inc=(ch == 7))
            for c in range(2):
                kb.op("act", lambda e, c=c: e.activation(out=junk[:, 0:512], in_=self.ps[:, bb[c], :],
                                                         func=AF.Square, accum_out=self.stat[:, 32 + c:33 + c]),
                      reads=[self.pb[bb[c]]], writes=[b_junk, self.b_stat])
            rs = self.stat[:, 34:35]
            kb.op("dve", lambda e: e.tensor_tensor(out=rs, in0=self.stat[:, 32:33], in1=self.stat[:, 33:34], op=ALU.add),
                  reads=[self.b_stat], writes=[self.b_stat])
            self.rsqrt(rs, rs, 1.0 / D, RMS_EPS)
            for c in range(2):
                kb.op("dve", lambda e, c=c: e.scalar_tensor_tensor(
                    out=tmp[:, c * 512:(c + 1) * 512], in0=self.ps[:, bb[c], :], scalar=rs,
                    in1=self.gbc[:, c * 512:(c + 1) * 512], op0=ALU.mult, op1=ALU.mult),
                    reads=[self.pb[bb[c]], self.b_stat, self.b_gbc], writes=[b_tmp])
            kb.op("dve", lambda e, tile=tile: e.tensor_tensor(
                out=self.xs[:, tile, :], in0=self.xs[:, tile, :], in1=tmp, op=ALU.add),
                reads=[b_tmp, self.xb[tile]], writes=[self.xb[tile]])

    def even_mixer(self, l):
        kb = self.kb
        i = l // 2
        kb.barrier()
        hT = self.av(0, [128, KC, S], BF16); b_hT = Buf("hT")
        qT = self.av(16384, [128, 4, S], BF16); b_qT = [Buf(f"qT{c}") for c in range(4)]
        kT = self.av(24576, [128, 4, S], BF16); b_kT = [Buf(f"kT{c}") for c in range(4)]
        vtok = self.av(32768, [128, NT, 512], BF16); b_v = Buf("vtok")
        cosT = self.av(40960, [128, S], F32)
        sinT = self.av(45056, [128, S], F32)
        b_cs = Buf("cs")
        t1 = self.av(49152, [128, 512], F32); t2 = self.av(50176, [128, 512], F32)
        b_t1, b_t2 = Buf("t1"), Buf("t2")
        xn = self.av(16384, [128, D], BF16); junk = self.av(16384 + D, [128, D], BF16)
        b_xn, b_junk = Buf("xn"), Buf("junk")
        kb.dma("sp", cosT, self.rope_d[0], self.sem_g, writes=[b_cs])
        kb.dma("sp", sinT, self.rope_d[1], self.sem_g, writes=[b_cs])
        self.prenorm_all(l, 2, hT, b_hT, xn, b_xn, junk, b_junk)
        kb.barrier()
        for grp in range(2):
            dst, b_dst = (qT, b_qT) if grp == 0 else (kT, b_kT)
            for p in range(2):
                wm, b_wm = self.ws.acquire((128, KC, 256))
                wp, b_wp = self.ws.acquire((128, KC, 256))
                for fc in range(2):
                    c = 2 * p + fc
                    for tg in range(4):
                        ba = self.next_bank(); bp = self.next_bank()
                        self.mm_fm(ba, wm, b_wm, fc, hT, b_hT, tg)
                        self.mm_fm(bp, wp, b_wp, fc, hT, b_hT, tg)
                        sl = slice(tg * 512, (tg + 1) * 512)
                        kb.op("dve", lambda e, ba=ba, sl=sl: e.tensor_tensor(out=t1, in0=self.ps[:, ba, :], in1=cosT[:, sl], op=ALU.mult),
                              reads=[self.pb[ba], b_cs], writes=[b_t1])
                        kb.op("dve", lambda e, bp=bp, sl=sl: e.tensor_tensor(out=t2, in0=self.ps[:, bp, :], in1=sinT[:, sl], op=ALU.mult),
                              reads=[self.pb[bp], b_cs], writes=[b_t2])
                        kb.op("dve", lambda e, c=c, sl=sl, dst=dst: e.tensor_tensor(out=dst[:, c, sl], in0=t1, in1=t2, op=ALU.add),
                              reads=[b_t1, b_t2], writes=[b_dst[c]])
        sv = [self.ws.acquire((128, KC, 256)) for _ in range(2)]
        for tile in range(NT):
            bk = self.next_bank()
            self.mm_tm(bk, [sv[0][0], sv[1][0]], [sv[0][1], sv[1][1]], hT, b_hT, tile)
            kb.op("act", lambda e, bk=bk, tile=tile: e.activation(out=vtok[:, tile, :], in_=self.ps[:, bk, :], func=AF.Copy),
                  reads=[self.pb[bk]], writes=[b_v])
        kb.barrier()
        HC = 30 + S
        hconv = self.av(40960, [128, 4, HC], BF16); b_hc = [Buf(f"hc{c}") for c in range(4)]
        sig = self.av(40960 + 4 * HC, [128, 512], F32); b_sig = Buf("sig")
        for c in range(4):
            kb.op("dve", lambda e, c=c: e.memset(hconv[:, c, 0:30], 0.0), writes=[b_hc[c]])
        for p in range(2):
            wa, b_wa = self.ws.acquire((128, KC, 256))
            wb, b_wb = self.ws.acquire((128, KC, 256))
            for fc in range(2):
                c = 2 * p + fc
                for tg in range(4):
                    ba = self.next_bank(); bg = self.next_bank()
                    self.mm_fm(ba, wa, b_wa, fc, hT, b_hT, tg)
                    self.mm_fm(bg, wb, b_wb, fc, hT, b_hT, tg)
                    kb.op("act", lambda e, bg=bg: e.activation(out=sig, in_=self.ps[:, bg, :], func=AF.Sigmoid),
                          reads=[self.pb[bg]], writes=[b_sig])
                    kb.op("dve", lambda e, ba=ba, c=c, tg=tg: e.tensor_tensor(
                        out=hconv[:, c, 30 + tg * 512:30 + (tg + 1) * 512], in0=self.ps[:, ba, :], in1=sig, op=ALU.mult),
                        reads=[self.pb[ba], b_sig], writes=[b_hc[c]])
        kb.barrier()
        oaT = self.av(0, [128, 4, S], BF16); b_oa = [Buf(f"oa{c}") for c in range(4)]
        mb_all = self.av(8192, [128, NT, 64], BF16); b_mb = Buf("mb")
        gate_sb = self.av(9216, [128, 64], F32); b_gate = Buf("gate")
        cmp = self.av(9344, [128, 512], F32); b_cmp = Buf("cmp")
        cnt = self.av(10368, [128, 64], F32); b_cnt = Buf("cnt")
        mbt = [self.arena[0:8, 10496 + k * 2048:10496 + (k + 1) * 2048] for k in range(2)]
        b_mbt = [Buf("mbt0"), Buf("mbt1")]
        pt = [self.av(14592 + k * 512, [128, 512], BF16) for k in range(2)] + \
             [self.av(9344 + k * 512, [128, 512], BF16) for k in range(2)]
        b_pt = [Buf(f"pt{k}") for k in range(4)]
        ptc = [0]
        rec = self.av(40960 + 4 * HC, [128, 512], F32); b_rec = Buf("rec")
        km32 = self.small[:, 0:32]; kmT = self.small[:, 32:48].bitcast(BF16)
        ident = self.cb[:, C_IDENT:C_IDENT + 128]
        for c in range(4):
            kb.op("dve", lambda e, c=c: e.tensor_reduce(
                out=km32[:, c * 8:(c + 1) * 8], in_=kT[:, c, :].rearrange("p (n t) -> p n t", t=256),
                axis=AX.X, op=ALU.add), reads=[b_kT[c]], writes=[self.b_small])
        kb.op("dve", lambda e: e.tensor_scalar(out=kmT, in0=km32, scalar1=1.0 / 256, scalar2=None, op0=ALU.mult),
              reads=[self.b_small], writes=[self.b_small])
        for tile in range(NT):
            own = tile // 2
            mbv = mb_all[:, tile, :].rearrange("p (h n) -> p h n", h=8)
            kb.op("dve", lambda e, tile=tile: e.memset(mb_all[:, tile, :], NEG_BIG), writes=[b_mb])
            kb.op("dve", lambda e, mbv=mbv, own=own: e.memset(mbv[:, :, 0:own + 1], 0.0), writes=[b_mb])
            if own <= 3:
                continue
            bk = self.next_bank()
            for h in range(8):
                c, pb_ = h // 2, 64 * (h % 2)
                kb.op("pe", lambda e, h=h, c=c, pb_=pb_: e.matmul(
                    self.ps[:, bk, h * 8:(h + 1) * 8], lhsT=qT[pb_:pb_ + 64, c, tile * 128:(tile + 1) * 128],
                    rhs=kmT[pb_:pb_ + 64, c * 8:(c + 1) * 8], start=True, stop=True),
                    reads=[b_qT[c], self.b_small], writes=[self.pb[bk]], inc=(h == 7))
            kb.op("act", lambda e, bk=bk: e.activation(out=gate_sb, in_=self.ps[:, bk, 0:64], func=AF.Copy),
                  reads=[self.pb[bk]], writes=[b_gate])
            g3 = gate_sb.rearrange("p (h n) -> p h n", h=8)[:, :, 0:own]
            cmpv = cmp[:, 0:8 * own * own].rearrange("p (h n m) -> p h n m", h=8, n=own)
            kb.op("dve", lambda e, g3=g3, cmpv=cmpv, own=own: e.tensor_tensor(
                out=cmpv, in0=g3.unsqueeze(2).to_broadcast([128, 8, own, own]),
                in1=g3.unsqueeze(3).to_broadcast([128, 8, own, own]), op=ALU.is_gt),
                reads=[b_gate], writes=[b_cmp])
            cntv = cnt[:, 0:8 * own].rearrange("p (h n) -> p h n", h=8)
            kb.op("dve", lambda e, cmpv=cmpv, cntv=cntv: e.tensor_reduce(out=cntv, in_=cmpv, axis=AX.X, op=ALU.add),
                  reads=[b_cmp], writes=[b_cnt])
            kb.op("dve", lambda e, cntv=cntv, mbv=mbv, own=own: e.tensor_scalar(
                out=mbv[:, :, 0:own], in0=cntv, scalar1=3.0, scalar2=NEG_BIG, op0=ALU.is_ge, op1=ALU.mult),
                reads=[b_cnt], writes=[b_mb])
        ones = self.cb[:, C_ONES:C_ONES + 128]
        tri = self.cb[:, C_TRI_LE:C_TRI_LE + 128]
        for h in range(8):
            c, pb_ = h // 2, 64 * (h % 2)
            mt, b_mt = mbt[h % 2], b_mbt[h % 2]
            for qg in range(2, 4):
                bk = self.next_bank()
                pst = self.bank(bk).bitcast(BF16)
                for k4 in range(4):
                    tile = 4 * qg + k4
                    kb.op("pe", lambda e, k4=k4, tile=tile: e.transpose(
                        out=pst[0:8, k4 * 128:(k4 + 1) * 128], in_=mb_all[:, tile, h * 8:(h + 1) * 8], identity=ident),
                        reads=[b_mb, self.b_const], writes=[self.pb[bk]], inc=(k4 == 3))
                kb.op("act", lambda e, qg=qg, pst=pst, mt=mt: e.activation(out=mt[:, qg * 512:(qg + 1) * 512], in_=pst[0:8, 0:512], func=AF.Copy),
                      reads=[self.pb[bk]], writes=[b_mt])
            for qg in range(4):
                bN = self.next_bank(hold=True); bD = self.next_bank(hold=True)
                kts = list(range(4 * qg + 4))
                info = {}

                def stage_s(kt):
                    q0 = max(qg * 512, kt * 128); off = q0 - qg * 512
                    bS = self.next_bank()
                    p_ = pt[ptc[0] % 4]; b_p = b_pt[ptc[0] % 4]; ptc[0] += 1
                    info[kt] = (off, q0, p_, b_p)
                    n = kt // 2
                    need_mask = (qg >= 2 and kt <= 4 * qg + 1)
                    kb.op("pe", lambda e: e.matmul(self.ps[:, bS, off:512], lhsT=kT[pb_:pb_ + 64, c, kt * 128:(kt + 1) * 128],
                                                   rhs=qT[pb_:pb_ + 64, c, q0:(qg + 1) * 512], start=True, stop=not need_mask),
                          reads=[b_kT[c], b_qT[c]], writes=[self.pb[bS]], inc=not need_mask)
                    if need_mask:
                        kb.op("pe", lambda e: e.matmul(self.ps[:, bS, off:512], lhsT=self.cb[0:8, C_OH + n * 128:C_OH + (n + 1) * 128],
                                                       rhs=mt[:, q0:(qg + 1) * 512], start=False, stop=True),
                              reads=[self.b_const, b_mt], writes=[self.pb[bS]])
                    kb.op("act", lambda e: e.activation(out=p_[:, off:512], in_=self.ps[:, bS, off:512], func=AF.Exp, scale=0.125),
                          reads=[self.pb[bS]], writes=[b_p])
                    if kt >= 4 * qg:
                        kb.op("dve", lambda e: e.tensor_tensor(out=p_[:, off:off + 128], in0=p_[:, off:off + 128], in1=tri, op=ALU.mult),
                              reads=[b_p, self.b_const], writes=[b_p])

                def stage_pv(kt):
                    off, q0, p_, b_p = info[kt]
                    last = (kt == kts[-1])
                    kb.op("pe", lambda e: e.matmul(self.ps[:, bN, off:512], lhsT=vtok[:, kt, c * 128:(c + 1) * 128],
                                                   rhs=p_[:, off:512], start=(kt == 0), stop=last),
                          reads=[b_v, b_p], writes=[self.pb[bN]], inc=last)
                    kb.op("pe", lambda e: e.matmul(self.ps[:, bD, off:512], lhsT=ones, rhs=p_[:, off:512],
                                                   start=(kt == 0), stop=last),
                          reads=[self.b_const, b_p], writes=[self.pb[bD]], inc=True)

                for idx in range(len(kts) + 2):
                    if idx < len(kts):
                        stage_s(kts[idx])
                    if idx >= 2:
                        stage_pv(kts[idx - 2])
                kb.op("dve", lambda e: e.reciprocal(out=rec[pb_:pb_ + 64, :], in_=self.ps[pb_:pb_ + 64, bD, :]),
                      reads=[self.pb[bD]], writes=[b_rec])
                kb.op("dve", lambda e: e.tensor_tensor(out=oaT[pb_:pb_ + 64, c, qg * 512:(qg + 1) * 512],
                                                       in0=self.ps[pb_:pb_ + 64, bN, :], in1=rec[pb_:pb_ + 64, :], op=ALU.mult),
                      reads=[self.pb[bN], b_rec], writes=[b_oa[c]])
                self.held.discard(bN); self.held.discard(bD)
        kb.barrier()
        acc = self.av(16384, [128, 4, S], F32); b_acc = [Buf(f"acc{c}") for c in range(4)]
        obT = self.av(32768, [128, 4, S], BF16); b_ob = [Buf(f"ob{c}") for c in range(4)]
        mean_t = self.av(8192, [128, 512], F32); rstd_t = self.av(9216, [128, 512], F32); yn = self.av(10240, [128, 512], F32)
        sq = [self.av(11264, [128, 512], F32), self.av(40960 + 4 * HC, [128, 512], F32)]
        b_mean, b_rstd, b_yn, b_sq = Buf("mean"), Buf("rstd"), Buf("yn"), [Buf("sq0"), Buf("sq1")]
        cw = self.small[:, 64:64 + CONV_FM]
        kb.dma("sp", cw, self.convfm_d[:, i * CONV_FM:(i + 1) * CONV_FM], self.sem_g, writes=[self.b_small])
        ones32 = self.small[:, 256:384]
        kb.op("dve", lambda e: e.memset(ones32, 1.0), writes=[self.b_small])
        dg = self.av(12288, [128, 31, 128], BF16); b_dg = Buf("dg")
        for c in range(4):
            for k in range(31):
                kb.op("dve", lambda e, c=c, k=k: e.tensor_scalar(out=dg[:, k, :], in0=ident, scalar1=cw[:, c * 31 + k:c * 31 + k + 1],
                                                                  scalar2=None, op0=ALU.mult),
                      reads=[self.b_const, self.b_small], writes=[b_dg])
            for tg in range(4):
                bk = self.next_bank()
                for k in range(31):
                    kb.op("pe", lambda e, c=c, k=k, tg=tg: e.matmul(
                        self.ps[:, bk, :], lhsT=dg[:, k, :], rhs=hconv[:, c, k + tg * 512:k + tg * 512 + 512],
                        start=(k == 0), stop=(k == 30)),
                        reads=[b_dg, b_hc[c]], writes=[self.pb[bk]], inc=(k == 30))
                kb.op("act", lambda e, c=c, tg=tg: e.activation(out=acc[:, c, tg * 512:(tg + 1) * 512], in_=self.ps[:, bk, :],
                                                                func=AF.Identity, bias=cw[:, 124 + c:125 + c], scale=1.0),
                      reads=[self.pb[bk], self.b_small], writes=[b_acc[c]])
        for tg in range(4):
            sl = slice(tg * 512, (tg + 1) * 512)
            b1 = self.next_bank(hold=True); b2 = self.next_bank(hold=True)
            for c in range(4):
                kb.op("pe", lambda e, c=c: e.matmul(self.ps[:, b1, :], lhsT=ones32, rhs=acc[:, c, sl], start=(c == 0), stop=(c == 3)),
                      reads=[self.b_small, b_acc[c]], writes=[self.pb[b1]], inc=(c == 3))
            for c in range(4):
                kb.op("act", lambda e, c=c: e.activation(out=sq[c % 2], in_=acc[:, c, sl], func=AF.Square),
                      reads=[b_acc[c]], writes=[b_sq[c % 2]])
                kb.op("pe", lambda e, c=c: e.matmul(self.ps[:, b2, :], lhsT=ones32, rhs=sq[c % 2], start=(c == 0), stop=(c == 3)),
                      reads=[self.b_small, b_sq[c % 2]], writes=[self.pb[b2]], inc=True)
            kb.op("dve", lambda e: e.tensor_scalar(out=mean_t, in0=self.ps[:, b1, :], scalar1=1.0 / 512, scalar2=None, op0=ALU.mult),
                  reads=[self.pb[b1]], writes=[b_mean])
            kb.op("dve", lambda e: e.tensor_tensor(out=yn, in0=mean_t, in1=mean_t, op=ALU.mult), reads=[b_mean], writes=[b_yn])
            kb.op("dve", lambda e: e.scalar_tensor_tensor(out=rstd_t, in0=self.ps[:, b2, :], scalar=1.0 / 512, in1=yn,
                                                          op0=ALU.mult, op1=ALU.subtract),
                  reads=[self.pb[b2], b_yn], writes=[b_rstd])
            kb.op("act", lambda e: e.activation(out=rstd_t, in_=rstd_t, func=AF.Sqrt, bias=self.eps_ap(LN_EPS), scale=1.0),
                  reads=[b_rstd, self.b_cf], writes=[b_rstd])
            kb.op("dve", lambda e: e.reciprocal(out=rstd_t, in_=rstd_t), reads=[b_rstd], writes=[b_rstd])
            self.held.discard(b1); self.held.discard(b2)
            for c in range(4):
                kb.op("dve", lambda e, c=c: e.tensor_tensor(out=yn, in0=acc[:, c, sl], in1=mean_t, op=ALU.subtract),
                      reads=[b_acc[c], b_mean], writes=[b_yn])
                kb.op("dve", lambda e: e.tensor_tensor(out=yn, in0=yn, in1=rstd_t, op=ALU.mult), reads=[b_yn, b_rstd], writes=[b_yn])
                kb.op("act", lambda e, c=c: e.activation(out=obT[:, c, sl], in_=yn, func=AF.Silu,
                                                         bias=cw[:, 132 + c:133 + c], scale=cw[:, 128 + c:129 + c]),
                      reads=[b_yn, self.b_small], writes=[b_ob[c]])
        tmp = self.av(8192, [128, D], F32); b_tmp = Buf("tmp")
        junk2 = self.av(12288, [128, 512], BF16); b_junk2 = Buf("junk2")
        kb.barrier()
        srcs = [oaT[:, c, :] for c in range(4)] + [obT[:, c, :] for c in range(4)]
        self.out_proj(l, srcs, b_oa + b_ob, tmp, b_tmp, junk2, b_junk2)

    def gelu(self, bk, dst, b_dst, ta, b_ta, tb, b_tb):
        kb = self.kb
        src = self.ps[:, bk, :]
        kb.op("act", lambda e: e.activation(out=ta, in_=src, func=AF.Square), reads=[self.pb[bk]], writes=[b_ta])
        kb.op("dve", lambda e: e.tensor_scalar(out=ta, in0=ta, scalar1=0.044715, scalar2=1.0, op0=ALU.mult, op1=ALU.add),
              reads=[b_ta], writes=[b_ta])
        kb.op("dve", lambda e: e.tensor_tensor(out=ta, in0=src, in1=ta, op=ALU.mult), reads=[b_ta, self.pb[bk]], writes=[b_ta])
        kb.op("act", lambda e: e.activation(out=tb, in_=ta, func=AF.Sigmoid, scale=1.5957691216057308),
              reads=[b_ta], writes=[b_tb])
        kb.op("dve", lambda e: e.tensor_tensor(out=dst, in0=src, in1=tb, op=ALU.mult), reads=[b_tb, self.pb[bk]], writes=[b_dst])

    def odd_mixer(self, l):
        kb = self.kb
        i = l // 2
        kb.barrier()
        hT = self.av(0, [128, KC, S], BF16); b_hT = Buf("hT")
        uT = self.av(16384, [128, 4, S], BF16); b_u = [Buf(f"u{c}") for c in range(4)]
        vln = self.av(24576, [128, NT, 512], BF16); b_vln = Buf("vln")
        ocT = self.av(32768, [128, 4, S], BF16); b_oc = [Buf(f"oc{c}") for c in range(4)]
        ta = self.av(40960, [128, 512], F32); tb = self.av(41984, [128, 512], F32); g32 = self.av(43008, [128, 512], F32)
        sgt = self.av(44032, [128, 512], F32)
        wmT = self.av(45056, [128, 4, 128], BF16)
        bias_bc = self.av(45568, [128, 512], F32)
        lng_bc = self.av(46592, [128, 512], F32); lnb_bc = self.av(47616, [128, 512], F32)
        w32 = self.av(48640, [128, 4, 128], F32)
        b_ta, b_tb, b_g32, b_sgt, b_wm, b_par = Buf("ta"), Buf("tb"), Buf("g32"), Buf("sgt"), Buf("wm"), Buf("par")
        xn = self.av(16384, [128, D], BF16); junk = self.av(16384 + D, [128, D], BF16)
        b_xn, b_junk = Buf("xn"), Buf("junk")
        kb.dma("sp", w32.rearrange("p a b -> p (a b)"), self.sguw_d[i], self.sem_g, writes=[b_par])
        kb.dma("sp", bias_bc, self.sgub_d[i:i + 1, :].to_broadcast([128, 512]), self.sem_g, writes=[b_par])
        kb.dma("sp", lng_bc, self.sguln_d[2 * i:2 * i + 1, :].to_broadcast([128, 512]), self.sem_g, writes=[b_par])
        kb.dma("sp", lnb_bc, self.sguln_d[2 * i + 1:2 * i + 2, :].to_broadcast([128, 512]), self.sem_g, writes=[b_par])
        tri_le = self.cb[:, C_TRI_LE:C_TRI_LE + 128]
        kb.op("dve", lambda e: e.tensor_tensor(out=wmT, in0=w32, in1=tri_le.unsqueeze(1).to_broadcast([128, 4, 128]), op=ALU.mult),
              reads=[b_par, self.b_const], writes=[b_wm])
        self.prenorm_all(l, 2, hT, b_hT, xn, b_xn, junk, b_junk)
        kb.barrier()
        for p in range(2):
            wu_, b_wu_ = self.ws.acquire((128, KC, 256))
            for fc in range(2):
                c = 2 * p + fc
                for tg in range(4):
                    bk = self.next_bank()
                    self.mm_fm(bk, wu_, b_wu_, fc, hT, b_hT, tg)
                    self.gelu(bk, uT[:, c, tg * 512:(tg + 1) * 512], b_u[c], ta, b_ta, tb, b_tb)
        sv = [self.ws.acquire((128, KC, 256)) for _ in range(2)]
        st = self.stat
        for tile in range(NT):
            bk = self.next_bank()
            self.mm_tm(bk, [sv[0][0], sv[1][0]], [sv[0][1], sv[1][1]], hT, b_hT, tile)
            self.gelu(bk, g32, b_g32, ta, b_ta, tb, b_tb)
            kb.op("dve", lambda e: e.tensor_reduce(out=st[:, 40:41], in_=g32, axis=AX.X, op=ALU.add),
                  reads=[b_g32], writes=[self.b_stat])
            kb.op("act", lambda e: e.activation(out=ta, in_=g32, func=AF.Square, accum_out=st[:, 41:42]),
                  reads=[b_g32], writes=[b_ta, self.b_stat])
            kb.op("dve", lambda e: e.tensor_scalar(out=st[:, 42:43], in0=st[:, 40:41], scalar1=1.0 / 512, scalar2=None, op0=ALU.mult),
                  reads=[self.b_stat], writes=[self.b_stat])
            kb.op("dve", lambda e: e.tensor_tensor(out=st[:, 43:44], in0=st[:, 42:43], in1=st[:, 42:43], op=ALU.mult),
                  reads=[self.b_stat], writes=[self.b_stat])
            kb.op("dve", lambda e: e.scalar_tensor_tensor(out=st[:, 44:45], in0=st[:, 41:42], scalar=1.0 / 512, in1=st[:, 43:44],
                                                          op0=ALU.mult, op1=ALU.subtract),
                  reads=[self.b_stat], writes=[self.b_stat])
            self.rsqrt(st[:, 44:45], st[:, 44:45], 1.0, LN_EPS)
            kb.op("dve", lambda e: e.tensor_scalar(out=g32, in0=g32, scalar1=st[:, 42:43], scalar2=st[:, 44:45],
                                                   op0=ALU.subtract, op1=ALU.mult),
                  reads=[b_g32, self.b_stat], writes=[b_g32])
            kb.op("dve", lambda e: e.tensor_tensor(out=g32, in0=g32, in1=lng_bc, op=ALU.mult), reads=[b_g32, b_par], writes=[b_g32])
            kb.op("dve", lambda e, tile=tile: e.tensor_tensor(out=vln[:, tile, :], in0=g32, in1=lnb_bc, op=ALU.add),
                  reads=[b_g32, b_par], writes=[b_vln])
        for g in range(4):
            for tg in range(4):
                bk = self.next_bank()
                for k4 in range(4):
                    kb.op("pe", lambda e, k4=k4: e.matmul(
                        self.ps[:, bk, k4 * 128:(k4 + 1) * 128], lhsT=vln[:, 4 * tg + k4, g * 128:(g + 1) * 128],
                        rhs=wmT[:, g, :], start=True, stop=True),
                        reads=[b_vln, b_wm], writes=[self.pb[bk]], inc=(k4 == 3))
                kb.op("dve", lambda e: e.tensor_tensor(
                    out=sgt.rearrange("p (a b) -> p a b", a=4), in0=self.ps[:, bk, :].rearrange("p (a b) -> p a b", a=4),
                    in1=bias_bc[:, g * 128:(g + 1) * 128].unsqueeze(1).to_broadcast([128, 4, 128]), op=ALU.add),
                    reads=[self.pb[bk], b_par], writes=[b_sgt])
                kb.op("dve", lambda e: e.tensor_tensor(out=ocT[:, g, tg * 512:(tg + 1) * 512], in0=sgt,
                                                       in1=uT[:, g, tg * 512:(tg + 1) * 512], op=ALU.mult),
                      reads=[b_sgt, b_u[g]], writes=[b_oc[g]])
        kb.barrier()
        qT = self.av(16384, [128, 4, S], BF16); b_qT = [Buf(f"qT{c}") for c in range(4)]
        kT = self.av(24576, [128, 4, S], BF16); b_kT = [Buf(f"kT{c}") for c in range(4)]
        vtok = self.av(40960, [128, NT, 512], BF16); b_v = Buf("vtok")
        for grp in range(2):
            dst, b_dst = (qT, b_qT) if grp == 0 else (kT, b_kT)
            for p in range(2):
                wm_, b_wm_ = self.ws.acquire((128, KC, 256))
                for fc in range(2):
                    c = 2 * p + fc
                    for tg in range(4):
                        bk = self.next_bank()
                        self.mm_fm(bk, wm_, b_wm_, fc, hT, b_hT, tg)
                        kb.op("act", lambda e, c=c, tg=tg, dst=dst: e.activation(
                            out=dst[:, c, tg * 512:(tg + 1) * 512], in_=self.ps[:, bk, :], func=AF.Copy,
                            scale=(1.0 if grp == 0 else 0.125)), reads=[self.pb[bk]], writes=[b_dst[c]])
        sv = [self.ws.acquire((128, KC, 256)) for _ in range(2)]
        for tile in range(NT):
            bk = self.next_bank()
            self.mm_tm(bk, [sv[0][0], sv[1][0]], [sv[0][1], sv[1][1]], hT, b_hT, tile)
            kb.op("act", lambda e, tile=tile: e.activation(out=vtok[:, tile, :], in_=self.ps[:, bk, :], func=AF.Copy),
                  reads=[self.pb[bk]], writes=[b_v])
        kb.barrier()
        odT = self.av(0, [128, 4, S], BF16); b_od = [Buf(f"od{c}") for c in range(4)]
        fb = [self.av(8192 + k * 1024, [128, 512], F32) for k in range(4)]
        spbf = [self.av(12288 + k * 512, [128, 512], BF16) for k in range(4)]
        att = [self.av(14336 + k * 512, [128, 512], BF16) for k in range(4)]
        b_fb, b_spbf, b_att = ([Buf(f"{n}{k}") for k in range(4)] for n in ("fb", "spbf", "att"))
        zeros = self.cb[:, C_ZEROS:C_ZEROS + 128]
        uneg = self.cb[:, C_UNEG:C_UNEG + 128]
        onesneg = self.cb[:, C_ONESNEG:C_ONESNEG + 128]
        tri_lt = self.cb[:, C_TRI_LT:C_TRI_LT + 128]
        one_ap = self.eps_ap(1.0)
        tiles = []
        for h in range(8):
            for qg in range(4):
                kts = list(range(4 * qg + 3, -1, -1))
                for kt in kts:
                    tiles.append((h, qg, kt, kt == kts[0], kt == 0))
        n_t = len(tiles)
        gst = {}
        zb = {}

        def geom(i):
            h, qg, kt, first, last = tiles[i]
            q0 = max(qg * 512, kt * 128)
            return h, qg, kt, first, last, h // 2, 64 * (h % 2), i % 4, q0, q0 - qg * 512

        def st_z(i):
            h, qg, kt, first, last, c, pb_, par, q0, off = geom(i)
            if first:
                bN = self.next_bank(hold=True); bR = self.next_bank(hold=True)
                gst[(h, qg)] = (bN, bR)
                for bz in (bN, bR):
                    kb.op("pe", lambda e, bz=bz: e.matmul(self.ps[:, bz, :], lhsT=zeros, rhs=qT[:, c, 0:512], start=True, stop=False),
                          reads=[self.b_const, b_qT[c]], writes=[self.pb[bz]], inc=True)
            bZ = self.next_bank(hold=True)
            zb[i] = bZ
            kb.op("pe", lambda e: e.matmul(self.ps[:, bZ, off:512], lhsT=kT[pb_:pb_ + 64, c, kt * 128:(kt + 1) * 128],
                                           rhs=qT[pb_:pb_ + 64, c, q0:(qg + 1) * 512], start=True, stop=True),
                  reads=[b_kT[c], b_qT[c]], writes=[self.pb[bZ]], inc=True)

        def st_sp(i):
            h, qg, kt, first, last, c, pb_, par, q0, off = geom(i)
            bZ = zb[i]
            kb.op("act", lambda e: e.activation(out=fb[par][:, off:512], in_=self.ps[:, bZ, off:512], func=AF.Exp),
                  reads=[self.pb[bZ]], writes=[b_fb[par]])
            kb.op("act", lambda e: e.activation(out=fb[par][:, off:512], in_=fb[par][:, off:512], func=AF.Ln, bias=one_ap, scale=1.0),
                  reads=[b_fb[par], self.b_cf], writes=[b_fb[par]])

        def st_cast(i):
            h, qg, kt, first, last, c, pb_, par, q0, off = geom(i)
            kb.op("dve", lambda e: e.tensor_copy(out=spbf[par][:, off:512], in_=fb[par][:, off:512]),
                  reads=[b_fb[par]], writes=[b_spbf[par]])
            if kt >= 4 * qg:
                kb.op("dve", lambda e: e.tensor_tensor(out=spbf[par][:, off:off + 128], in0=spbf[par][:, off:off + 128],
                                                       in1=tri_lt, op=ALU.mult),
                      reads=[b_spbf[par], self.b_const], writes=[b_spbf[par]])

        def st_y(i):
            h, qg, kt, first, last, c, pb_, par, q0, off = geom(i)
            bZ = zb[i]
            kb.op("pe", lambda e: e.matmul(self.ps[:, bZ, off:512], lhsT=uneg, rhs=spbf[par][:, off:512], start=False, stop=True),
                  reads=[self.b_const, b_spbf[par]], writes=[self.pb[bZ]], inc=True)

        def st_sub(i):
            h, qg, kt, first, last, c, pb_, par, q0, off = geom(i)
            bZ = zb[i]
            kb.op("dve", lambda e: e.tensor_tensor(out=fb[par][:, off:512], in0=self.ps[:, bZ, off:512],
                                                   in1=fb[par][:, off:512], op=ALU.subtract),
                  reads=[self.pb[bZ], b_fb[par]], writes=[b_fb[par]])
            self.held.discard(bZ)

        def st_add(i):
            h, qg, kt, first, last, c, pb_, par, q0, off = geom(i)
            bN, bR = gst[(h, qg)]
            kb.op("dve", lambda e: e.tensor_tensor(out=fb[par][:, off:512], in0=self.ps[:, bR, off:512],
                                                   in1=fb[par][:, off:512], op=ALU.add),
                  reads=[self.pb[bR], b_fb[par]], writes=[b_fb[par]])
            kb.op("pe", lambda e: e.matmul(self.ps[:, bR, off:512], lhsT=onesneg, rhs=spbf[par][:, off:512], start=False, stop=False),
                  reads=[self.b_const, b_spbf[par]], writes=[self.pb[bR]], inc=True)

        def st_att(i):
            h, qg, kt, first, last, c, pb_, par, q0, off = geom(i)
            kb.op("act", lambda e: e.activation(out=att[par][:, off:512], in_=fb[par][:, off:512], func=AF.Exp),
                  reads=[b_fb[par]], writes=[b_att[par]])
            if kt >= 4 * qg:
                kb.op("dve", lambda e: e.tensor_tensor(out=att[par][:, off:off + 128], in0=att[par][:, off:off + 128],
                                                       in1=tri_lt, op=ALU.mult),
                      reads=[b_att[par], self.b_const], writes=[b_att[par]])

        def st_pv(i):
            h, qg, kt, first, last, c, pb_, par, q0, off = geom(i)
            bN, bR = gst[(h, qg)]
            kb.op("pe", lambda e: e.matmul(self.ps[:, bN, off:512], lhsT=vtok[:, kt, c * 128:(c + 1) * 128],
                                           rhs=att[par][:, off:512], start=False, stop=last),
                  reads=[b_v, b_att[par]], writes=[self.pb[bN]], inc=True)
            if last:
                kb.op("dve", lambda e: e.tensor_copy(out=odT[pb_:pb_ + 64, c, qg * 512:(qg + 1) * 512], in_=self.ps[pb_:pb_ + 64, bN, :]),
                      reads=[self.pb[bN]], writes=[b_od[c]])
                self.held.discard(bN); self.held.discard(bR)

        for s_ in range(n_t + 4):
            if 0 <= s_ - 2 < n_t:
                st_y(s_ - 2)
            if s_ < n_t:
                st_z(s_)
            if 0 <= s_ - 1 < n_t:
                st_sp(s_ - 1)
            if 0 <= s_ - 3 < n_t:
                st_add(s_ - 3)
            if 0 <= s_ - 2 < n_t:
                st_sub(s_ - 2)
            if 0 <= s_ - 1 < n_t:
                st_cast(s_ - 1)
            if 0 <= s_ - 3 < n_t:
                st_att(s_ - 3)
            if 0 <= s_ - 4 < n_t:
                st_pv(s_ - 4)
        kb.barrier()
        tmp = self.av(8192, [128, D], F32); b_tmp = Buf("tmp")
        junk2 = self.av(12288, [128, 512], BF16); b_junk2 = Buf("junk2")
        srcs = [ocT[:, c, :] for c in range(4)] + [odT[:, c, :] for c in range(4)]
        self.out_proj(l, srcs, b_oc + b_od, tmp, b_tmp, junk2, b_junk2)


C_IDENT = 0
C_TRI_LE = 128
C_TRI_LT = 256
C_ONES = 384
C_UNEG = 512
C_ONESNEG = 640
C_ZEROS = 768
C_OH = 896
CB_COLS = 896 + 1024
CONV_FM = 4 * 31 + 12


def make_consts():
    cb = np.zeros((128, CB_COLS), np.float32)
    cb[:, C_IDENT:C_IDENT + 128] = np.eye(128, dtype=np.float32)
    j = np.arange(128)[:, None]
    t = np.arange(128)[None, :]
    cb[:, C_TRI_LE:C_TRI_LE + 128] = (j <= t)
    cb[:, C_TRI_LT:C_TRI_LT + 128] = (j < t)
    cb[:, C_ONES:C_ONES + 128] = 1.0
    cb[:, C_UNEG:C_UNEG + 128] = -(j > t).astype(np.float32)
    cb[:, C_ONESNEG:C_ONESNEG + 128] = -1.0
    for n in range(8):
        cb[n, C_OH + n * 128:C_OH + (n + 1) * 128] = 1.0
    return cb


def rope_tables():
    pos = np.arange(S, dtype=np.float32)
    inv = (np.float32(10000.0) ** (-np.arange(0, 64, 2, dtype=np.float32) / np.float32(64))).astype(np.float32)
    ang = (pos[:, None] * inv[None, :]).astype(np.float32)
    cos = np.cos(ang).astype(np.float32).T
    sin = np.sin(ang).astype(np.float32).T
    cs = np.zeros((2, 128, S), np.float32)
    for hh in range(2):
        cs[0, hh * 64:hh * 64 + 32] = cos
        cs[0, hh * 64 + 32:hh * 64 + 64] = cos
        cs[1, hh * 64:hh * 64 + 32] = -sin
        cs[1, hh * 64 + 32:hh * 64 + 64] = sin
    return cs


def host_inputs(inputs, b):
    ng = np.asarray(inputs["norm_g"], np.float32)
    g_fm = np.ascontiguousarray(ng.reshape(DEPTH * 6, KC, 128).transpose(2, 0, 1).reshape(128, DEPTH * 6 * KC))
    m = {
        "x": np.ascontiguousarray(inputs["x"][b]),
        "g_fm": g_fm,
        "norm_g": ng,
        "ffn_w_gate": inputs["ffn_w_gate"],
        "ffn_w_up": inputs["ffn_w_up"],
        "ffn_w_down": inputs["ffn_w_down"],
        "consts_bf": make_consts(),
    }
    w = np.asarray(inputs["ab_w_in"], np.float32)
    perm = np.concatenate([np.arange(h * 64, h * 64 + 64).reshape(2, 32)[::-1].reshape(-1) for h in range(8)])
    m["ab_w_in_ext"] = np.ascontiguousarray(
        np.concatenate([w, w[:, :, 0:512][:, :, perm], w[:, :, 512:1024][:, :, perm]], axis=2))
    m["ab_w_out"] = inputs["ab_w_out"]
    m["cd_w_in"] = inputs["cd_w_in"]
    m["cd_w_out"] = inputs["cd_w_out"]
    cf = np.zeros((128, 2 * CONV_FM), np.float32)
    for i in range(2):
        cw = np.asarray(inputs["conv_w"][i], np.float32)
        cf[:, i * CONV_FM:i * CONV_FM + 124] = cw.T.reshape(4, 128, 31).transpose(1, 0, 2).reshape(128, 124)
        for q, name in enumerate(("conv_b", "conv_ln_g", "conv_ln_b")):
            v = np.asarray(inputs[name][i], np.float32).reshape(4, 128).T
            cf[:, i * CONV_FM + 124 + 4 * q:i * CONV_FM + 128 + 4 * q] = v
    m["conv_fm"] = cf
    m["rope_cs"] = rope_tables()
    sw = np.asarray(inputs["sgu_w"], np.float32)
    m["sgu_wT"] = np.ascontiguousarray(sw.transpose(0, 3, 1, 2).reshape(2, 128, 512))
    m["sgu_b"] = np.ascontiguousarray(np.asarray(inputs["sgu_b"], np.float32).reshape(2, 512))
    m["sgu_ln"] = np.ascontiguousarray(np.stack([inputs["sgu_ln_g"][0], inputs["sgu_ln_b"][0],
                                                 inputs["sgu_ln_g"][1], inputs["sgu_ln_b"][1]]).astype(np.float32))
    return m


_PROG_CACHE = {}


def get_prog(layers, stages=None):
    key = (tuple(layers), None if stages is None else tuple(sorted(stages)))
    if key not in _PROG_CACHE:
        _PROG_CACHE[key] = Prog(layers, stages)
    return _PROG_CACHE[key]


def kernel(**inputs):
    inputs = {k: np.asarray(v) for k, v in inputs.items()}
    nb = inputs["x"].shape[0]
    prog = get_prog(range(DEPTH))
    in_maps = [host_inputs(inputs, b) for b in range(nb)]
    res = run_bass_kernel_spmd(prog.nc, in_maps, core_ids=list(range(nb)))
    return np.stack([np.asarray(r["out"]) for r in res.results], axis=0).astype(np.float32)
```

```python
import numpy as np
from contextlib import ExitStack
import concourse.bass as bass
import concourse.mybir as mybir
from concourse.bass_utils import run_bass_kernel_spmd

F32 = mybir.dt.float32
BF16 = mybir.dt.bfloat16
AF = mybir.ActivationFunctionType
ALU = mybir.AluOpType
AX = mybir.AxisListType

S = 2048
D = 1024
DFF = 2816
NT = 16
KC = 8
NJ = 22
DEPTH = 4
RMS_EPS = 1e-6
LN_EPS = 1e-5
NSLOT = 8
NHOLD = 4
SLOT_ELEMS = 2048
ARENA_ELEMS = 50 * 1024
NEG_BIG = -30000.0


class Buf:
    __slots__ = ("name", "w", "rs")

    def __init__(self, name):
        self.name = name
        self.w = None
        self.rs = {}


class Sem:
    __slots__ = ("h", "name", "cnt")

    def __init__(self, h, name):
        self.h = h
        self.name = name
        self.cnt = 0


class KB:
    def __init__(self, nc, es):
        self.nc = nc
        self.es = es
        self.E = {"pe": nc.tensor, "act": nc.scalar, "dve": nc.vector, "pool": nc.gpsimd, "sp": nc.sync}
        self.sem = {e: self.new_sem("eng_" + e) for e in self.E}
        self.seen = {e: {} for e in self.E}
        self.nwait = 0
        self.nins = 0

    def new_sem(self, name):
        return Sem(self.es.enter_context(self.nc.semaphore(name)), name)

    def _deps(self, reads, writes):
        d = {}

        def add(t):
            if t is None:
                return
            s, v = t
            if s.name not in d or d[s.name][1] < v:
                d[s.name] = (s, v)

        for b in reads:
            add(b.w)
        for b in writes:
            add(b.w)
            for t in b.rs.values():
                add(t)
        return d

    def _wait(self, e, deps):
        own = self.sem[e]
        for name, (sem, val) in deps.items():
            if sem is own and (e == "pe" or val > sem.cnt):
                continue
            if self.seen[e].get(name, 0) >= val:
                continue
            self.E[e].wait_ge(sem.h, val)
            self.seen[e][name] = val
            self.nwait += 1

    def op(self, e, fn, reads=(), writes=(), inc=True):
        self._wait(e, self._deps(reads, writes))
        ins = fn(self.E[e])
        sem = self.sem[e]
        val = sem.cnt + 1
        if inc:
            ins.then_inc(sem.h, 1)
            sem.cnt = val
        t = (sem, val)
        for b in reads:
            b.rs[sem.name] = t
        for b in writes:
            b.w = t
            b.rs = {}
        self.nins += 1
        return ins

    def dma(self, q, out, in_, sem, reads=(), writes=()):
        self._wait(q, self._deps(reads, writes))
        ins = self.E[q].dma_start(out=out, in_=in_)
        ins.then_inc(sem.h, 16)
        sem.cnt += 16
        t = (sem, sem.cnt)
        for b in reads:
            b.rs[sem.name] = t
        for b in writes:
            b.w = t
            b.rs = {}
        self.nins += 1
        return ins

    def barrier(self, engines=("pe", "act", "dve", "sp")):
        for e in engines:
            for o in engines:
                if o == e or o == "sp":
                    continue
                s = self.sem[o]
                if s.cnt > self.seen[e].get(s.name, 0):
                    self.E[e].wait_ge(s.h, s.cnt)
                    self.seen[e][s.name] = s.cnt


class WStream:
    def __init__(self, kb, slots, jobs):
        self.kb = kb
        self.slots = slots
        self.jobs = jobs
        self.issued = 0
        self.next = 0

    def _issue(self, k):
        shape, src = self.jobs[k]
        th, buf, sem = self.slots[k % NSLOT]
        n = 1
        for v in shape[1:]:
            n *= v
        dst = th[:, 0:n]
        if len(shape) == 3:
            dst = dst.rearrange("p (a b) -> p a b", a=shape[1])
        self.kb.dma("pool", dst, src, sem, writes=[buf])

    def acquire(self, shape, hold=NHOLD):
        k = self.next
        assert k < len(self.jobs), "weight job list exhausted"
        assert tuple(self.jobs[k][0]) == tuple(shape), (k, self.jobs[k][0], shape)
        while self.issued < min(len(self.jobs), k + NSLOT - hold + 1):
            self._issue(self.issued)
            self.issued += 1
        self.next += 1
        th, buf, sem = self.slots[k % NSLOT]
        n = 1
        for v in shape[1:]:
            n *= v
        v = th[:, 0:n]
        if len(shape) == 3:
            v = v.rearrange("p (a b) -> p a b", a=shape[1])
        return v, buf


class Prog:
    def __init__(self, layers, stages=None):
        self.layers = list(layers)
        self.stages = stages
        self.nc = bass.Bass("TRN2", target_bir_lowering=False)
        self.build()

    def dram_in(self, name, shape, dt=F32):
        return self.nc.dram_tensor(name, list(shape), dt, kind="ExternalInput").ap()

    def declare(self):
        self.x_d = self.dram_in("x", [S, D])
        self.out_d = self.nc.dram_tensor("out", [S, D], F32, kind="ExternalOutput").ap()
        self.gfm_d = self.dram_in("g_fm", [128, DEPTH * 6 * KC])
        self.gtm_d = self.dram_in("norm_g", [DEPTH, 6, D])
        self.wg_d = self.dram_in("ffn_w_gate", [DEPTH, 2, D, DFF])
        self.wu_d = self.dram_in("ffn_w_up", [DEPTH, 2, D, DFF])
        self.wd_d = self.dram_in("ffn_w_down", [DEPTH, 2, DFF, D])
        self.cb_d = self.dram_in("consts_bf", [128, CB_COLS])
        self.abin_d = self.dram_in("ab_w_in_ext", [2, D, 3584])
        self.about_d = self.dram_in("ab_w_out", [2, D, D])
        self.cdin_d = self.dram_in("cd_w_in", [2, D, 2560])
        self.cdout_d = self.dram_in("cd_w_out", [2, D, D])
        self.convfm_d = self.dram_in("conv_fm", [128, 2 * CONV_FM])
        self.rope_d = self.dram_in("rope_cs", [2, 128, S])
        self.sguw_d = self.dram_in("sgu_wT", [2, 128, 4 * 128])
        self.sgub_d = self.dram_in("sgu_b", [2, 512])
        self.sguln_d = self.dram_in("sgu_ln", [4, 512])

    def want(self, stage):
        return self.stages is None or stage in self.stages

    def ffn_jobs(self, l, i):
        jobs = []
        wg = self.wg_d[l, i].rearrange("(kc p) f -> p kc f", p=128)
        wu = self.wu_d[l, i].rearrange("(kc p) f -> p kc f", p=128)
        wd = self.wd_d[l, i].rearrange("(j p) d -> p j d", p=128)
        for G in range(2):
            for s in range(NJ // 2):
                jobs.append(((128, KC, 256), wg[:, :, s * 256:(s + 1) * 256]))
                jobs.append(((128, KC, 256), wu[:, :, s * 256:(s + 1) * 256]))
            for c in range(2):
                for sd in range(6):
                    nj = 4 if sd < 5 else 2
                    jobs.append(((128, nj, 512), wd[:, 4 * sd:4 * sd + nj, c * 512:(c + 1) * 512]))
        return jobs

    def av(self, off, shape, dt):
        n = 1
        for v in shape[1:]:
            n *= v
        units = n * (2 if dt == F32 else 1)
        assert off + units <= ARENA_ELEMS, (off, units)
        v = self.arena[:, off:off + units]
        if dt == F32:
            v = v.bitcast(F32)
        if len(shape) == 3:
            v = v.rearrange("p (a b) -> p a b", a=shape[1])
        return v

    def bank(self, b):
        return self.ps[:, b, :]

    def dbank(self, p):
        return self.ps[:, 2 * p:2 * p + 2, :]

    def build(self):
        nc = self.nc
        self.declare()
        with ExitStack() as es:
            kb = self.kb = KB(nc, es)
            self.xs = es.enter_context(nc.sbuf_tensor("xs", [128, NT, D], F32))
            self.arena = es.enter_context(nc.sbuf_tensor("arena", [128, ARENA_ELEMS], BF16))
            self.ps = es.enter_context(nc.psum_tensor("ps", [128, 8, 512], F32))
            self.cb = es.enter_context(nc.sbuf_tensor("cb", [128, CB_COLS], BF16))
            self.gfm = es.enter_context(nc.sbuf_tensor("gfm", [128, DEPTH * 6 * KC], F32))
            self.gbc = es.enter_context(nc.sbuf_tensor("gbc", [128, D], F32))
            self.stat = es.enter_context(nc.sbuf_tensor("stat", [128, 64], F32))
            self.cf = es.enter_context(nc.sbuf_tensor("cf", [128, 16], F32))
            self._eps = {}
            for v in (RMS_EPS, 4.0 * RMS_EPS, LN_EPS, 1.0):
                self.eps_ap(v)
            nc.vector.memset(self.stat[:, :], 0.0).then_inc(kb.sem["dve"].h, 1)
            kb.sem["dve"].cnt += 1
            self.b_cf = Buf("cf")
            self.b_cf.w = (kb.sem["dve"], 1)
            slots = []
            for k in range(NSLOT):
                th = es.enter_context(nc.sbuf_tensor(f"wslot{k}", [128, SLOT_ELEMS], BF16))
                slots.append((th, Buf(f"wslot{k}"), kb.new_sem(f"wsem{k}")))
            self.xb = [Buf(f"x{t}") for t in range(NT)]
            self.pb = [Buf(f"ps{b}") for b in range(8)]
            self.b_gbc = Buf("gbc")
            self.b_stat = Buf("stat")
            self.b_stat.w = (kb.sem["dve"], 1)
            self.b_const = Buf("const")
            self.sem_in = kb.new_sem("sem_in")
            self.sem_out = kb.new_sem("sem_out")
            self.sem_g = kb.new_sem("sem_g")

            jobs = []
            for l in self.layers:
                if self.want("ffn0"):
                    jobs += self.ffn_jobs(l, 0)
                if self.want("mix"):
                    jobs += self.mixer_jobs(l)
                if self.want("ffn1"):
                    jobs += self.ffn_jobs(l, 1)
            self.ws = WStream(kb, slots, jobs)

            self.sem_cb = kb.new_sem("sem_cb")
            kb.dma("pool", self.cb[:, :], self.cb_d[:, :], self.sem_cb, writes=[self.b_const])
            tot_cb = (self.sem_cb, self.sem_cb.cnt)
            self.b_gfm = Buf("gfm")
            kb.dma("sp", self.gfm[:, :], self.gfm_d[:, :], kb.new_sem("sem_gfm"), writes=[self.b_gfm])
            for t in range(NT):
                kb.dma("sp", self.xs[:, t, :], self.x_d[t * 128:(t + 1) * 128, :], kb.new_sem(f"sem_x{t}"),
                       writes=[self.xb[t]])
            self.b_const.w = tot_cb

            self.small = es.enter_context(nc.sbuf_tensor("small", [128, 512], F32))
            self.b_small = Buf("small")
            self.held = set()
            for l in self.layers:
                if self.want("ffn0"):
                    self.ffn(l, 0)
                if self.want("mix"):
                    if l % 2 == 0:
                        self.even_mixer(l)
                    else:
                        self.odd_mixer(l)
                if self.want("ffn1"):
                    self.ffn(l, 1)

            for t in range(NT):
                kb.dma("sp", self.out_d[t * 128:(t + 1) * 128, :], self.xs[:, t, :], self.sem_out,
                       reads=[self.xb[t]])
            nc.sync.wait_ge(self.sem_out.h, self.sem_out.cnt)
            print(f"[build] instructions={kb.nins} waits={kb.nwait}")

    def load_gbc(self, l, i):
        src = self.gtm_d[l, i:i + 1, :].to_broadcast([128, D])
        self.kb.dma("sp", self.gbc[:, :], src, self.sem_g, writes=[self.b_gbc])

    def prenorm_tile(self, l, i, tile, xn, b_xn, junk, b_junk, dst, b_dst, part=3):
        kb = self.kb
        if part == 2:
            return self.prenorm_p2(l, i, xn, b_xn, dst, b_dst)
        xt = self.xs[:, tile, :]
        ss = self.stat[:, 0:1]
        rs = self.stat[:, 1:2]
        kb.op("act", lambda e: e.activation(out=junk, in_=xt, func=AF.Square, accum_out=ss),
              reads=[self.xb[tile]], writes=[b_junk, self.b_stat])
        self.rsqrt(rs, ss, 1.0 / D, RMS_EPS)
        kb.op("dve", lambda e: e.tensor_scalar(out=xn, in0=xt, scalar1=rs, scalar2=None, op0=ALU.mult),
              reads=[self.xb[tile], self.b_stat], writes=[b_xn])
        if part == 1:
            return
        self.prenorm_p2(l, i, xn, b_xn, dst, b_dst)

    def prenorm_p2(self, l, i, xn, b_xn, dst, b_dst, bank=None):
        kb = self.kb
        bk = self.next_bank() if bank is None else bank
        pst = self.bank(bk).bitcast(BF16)
        ident = self.cb[:, C_IDENT:C_IDENT + 128]
        for kc in range(KC):
            kb.op("pe", lambda e, kc=kc: e.transpose(out=pst[:, kc * 128:(kc + 1) * 128],
                                                     in_=xn[:, kc * 128:(kc + 1) * 128], identity=ident),
                  reads=[b_xn, self.b_const], writes=[self.pb[bk]], inc=(kc == KC - 1))
        g = self.gfm[:, (l * 6 + i) * KC:(l * 6 + i + 1) * KC]
        kb.op("dve", lambda e: e.tensor_tensor(out=dst, in0=pst.rearrange("p (a b) -> p a b", a=KC),
                                               in1=g.unsqueeze(2).to_broadcast([128, KC, 128]), op=ALU.mult),
              reads=[self.pb[bk], self.b_gfm], writes=[b_dst])

    def rsqrt(self, out, in_, scale, bias):
        kb = self.kb
        kb.op("act", lambda e: e.activation(out=out, in_=in_, func=AF.Sqrt, bias=self.eps_ap(bias), scale=scale),
              reads=[self.b_stat, self.b_cf], writes=[self.b_stat])
        kb.op("dve", lambda e: e.reciprocal(out=out, in_=out), reads=[self.b_stat], writes=[self.b_stat])

    def eps_ap(self, val):
        if val not in self._eps:
            col = self.cf[:, len(self._eps):len(self._eps) + 1]
            self.nc.vector.memset(col, float(val))
            self._eps[val] = col
        return self._eps[val]

    def next_bank(self, hold=False):
        b = getattr(self, "_bank_rr", -1)
        for _ in range(8):
            b = (b + 1) % 8
            if b not in self.held:
                break
        self._bank_rr = b
        if hold:
            self.held.add(b)
        return b

    def ffn(self, l, i):
        kb = self.kb
        kb.barrier()
        dbg = 99
        if dbg < 1:
            return
        A = 0
        hT0 = self.av(A, [128, KC, 1024], BF16); A += KC * 1024
        hT1 = self.av(36864, [128, KC, 1024], BF16)
        hTs = [hT0, hT1]
        stashes = [self.av(0, [128, 8, 512], F32), self.av(36864, [128, 8, 512], F32)]
        xn1 = self.av(36864 + KC * 1024, [128, D], BF16); b_xn1 = Buf("xn1")
        hid = self.av(A, [128, NJ, 1024], BF16); A += NJ * 1024
        xn = self.av(A, [128, D], BF16); A += D
        junk = self.av(A, [128, D], BF16); A += D
        sg = self.av(A, [128, 1024], F32); A += 2048
        tmp = self.av(A, [128, 1024], F32); A += 2048
        b_hTs, b_xn, b_junk, b_sg, b_tmp = [Buf("hT0"), Buf("hT1")], Buf("xn"), Buf("junk"), Buf("sg"), Buf("tmp")
        b_hid = [Buf(f"hid{j}") for j in range(NJ)]
        ssf = self.stat[:, 8:8 + 16].rearrange("p (t c) -> p t c", c=2)
        b_ssf = Buf("ssf")
        self.load_gbc(l, 1 if i == 0 else 5)
        gi_pre = 0 if i == 0 else 4
        if dbg < 2:
            return
        deferred = []
        for G in range(2):
            hT, b_hT, stash = hTs[G], b_hTs[G], stashes[G]
            if G == 0:
                for t in range(8):
                    self.prenorm_tile(l, gi_pre, t, xn, b_xn, junk, b_junk, hT[:, :, t * 128:(t + 1) * 128], b_hT)
            xnb = [(xn, b_xn), (xn1, b_xn1)]
            if dbg < 3:
                return
            for s in range(NJ // 2):
                wgs, b_wg = self.ws.acquire((128, KC, 256), hold=2)
                wus, b_wu = self.ws.acquire((128, KC, 256), hold=2)
                for jj in range(2):
                    j = 2 * s + jj
                    pg = (2 * j) % 4
                    pu = pg + 1
                    gb = [self.pb[2 * pg], self.pb[2 * pg + 1]]
                    ub = [self.pb[2 * pu], self.pb[2 * pu + 1]]
                    for kc in range(KC):
                        if kc == 4 and G == 0 and j % 2 == 0 and 0 <= (j - 4) // 2 < 8:
                            t_ = (j - 4) // 2
                            xx, b_xx = xnb[t_ % 2]
                            self.prenorm_p2(l, gi_pre, xx, b_xx, hTs[1][:, :, t_ * 128:(t_ + 1) * 128], b_hTs[1], bank=7)
                        for h2 in range(2):
                            kb.op("pe", lambda e, kc=kc, h2=h2: e.matmul(
                                self.ps[:, 2 * pg + h2, :], lhsT=wgs[:, kc, jj * 128:(jj + 1) * 128],
                                rhs=hT[:, kc, h2 * 512:(h2 + 1) * 512], start=(kc == 0), stop=(kc == KC - 1)),
                                reads=[b_wg, b_hT], writes=[gb[h2]], inc=(kc == KC - 1))
                        for h2 in range(2):
                            kb.op("pe", lambda e, kc=kc, h2=h2: e.matmul(
                                self.ps[:, 2 * pu + h2, :], lhsT=wus[:, kc, jj * 128:(jj + 1) * 128],
                                rhs=hT[:, kc, h2 * 512:(h2 + 1) * 512], start=(kc == 0), stop=(kc == KC - 1)),
                                reads=[b_wu, b_hT], writes=[ub[h2]], inc=(kc == KC - 1))
                    if deferred:
                        deferred.pop(0)()
                    kb.op("act", lambda e: e.activation(out=sg.rearrange("p (a b) -> p a b", a=2),
                                                        in_=self.dbank(pg), func=AF.Silu),
                          reads=gb, writes=[b_sg])
                    kb.op("dve", lambda e: e.tensor_tensor(out=hid[:, j, :].rearrange("p (a b) -> p a b", a=2),
                                                           in0=self.dbank(pu),
                                                           in1=sg.rearrange("p (a b) -> p a b", a=2), op=ALU.mult),
                          reads=ub + [b_sg], writes=[b_hid[j]])
                    if G == 0 and j % 2 == 1 and (j - 1) // 2 < 8:
                        t_ = (j - 1) // 2
                        xx, b_xx = xnb[t_ % 2]
                        self.prenorm_tile(l, gi_pre, 8 + t_, xx, b_xx, junk, b_junk,
                                          hTs[1][:, :, t_ * 128:(t_ + 1) * 128], b_hTs[1], part=1)
            if dbg < 4:
                return
            skip = ''
            for c in range(2):
                for sd in range(6):
                    nj = 4 if sd < 5 else 2
                    wds, b_wd = self.ws.acquire((128, nj, 512), hold=2)
                    for jj in range(nj):
                        j = 4 * sd + jj
                        for t in range(8):
                            kb.op("pe", lambda e, t=t, jj=jj, j=j: e.matmul(
                                self.ps[:, t, :], lhsT=hid[:, j, t * 128:(t + 1) * 128], rhs=wds[:, jj, :],
                                start=(j == 0), stop=(j == NJ - 1)),
                                reads=[b_wd, b_hid[j]], writes=[self.pb[t]], inc=(j == NJ - 1 or (jj == nj - 1 and t == 7)))
                for t in range(8):
                    dstv = stash[:, t, :] if c == 0 else hid[:, t, :].bitcast(F32)
                    b_d = b_hT if c == 0 else b_hid[t]
                    kb.op("dve", lambda e, t=t, dstv=dstv: e.tensor_copy(out=dstv, in_=self.ps[:, t, :]),
                          reads=[self.pb[t]], writes=[b_d])
                    kb.op("act", lambda e, t=t, dstv=dstv: e.activation(out=junk[:, 0:512], in_=dstv, func=AF.Square,
                                                                        accum_out=self.stat[:, 8 + 2 * t + c:9 + 2 * t + c]),
                          reads=[b_d], writes=[b_junk, b_ssf])
                if c == 1:
                    for t in range(8):
                        def resid(t=t, tile=G * 8 + t, stash=stash, b_hT=b_hT):
                            rs = self.stat[:, 2:3]
                            kb.op("dve", lambda e: e.tensor_tensor(out=rs, in0=self.stat[:, 8 + 2 * t:9 + 2 * t],
                                                                   in1=self.stat[:, 9 + 2 * t:10 + 2 * t], op=ALU.add),
                                  reads=[b_ssf], writes=[self.b_stat])
                            self.rsqrt(rs, rs, 4.0 / D, 4.0 * RMS_EPS)
                            kb.op("dve", lambda e: e.scalar_tensor_tensor(
                                out=tmp[:, 0:512], in0=stash[:, t, :], scalar=rs, in1=self.gbc[:, 0:512],
                                op0=ALU.mult, op1=ALU.mult),
                                reads=[b_hT, self.b_stat, self.b_gbc], writes=[b_tmp])
                            kb.op("dve", lambda e: e.scalar_tensor_tensor(
                                out=tmp[:, 512:1024], in0=hid[:, t, :].bitcast(F32), scalar=rs, in1=self.gbc[:, 512:1024],
                                op0=ALU.mult, op1=ALU.mult),
                                reads=[b_hid[t], self.b_stat, self.b_gbc], writes=[b_tmp])
                            kb.op("dve", lambda e: e.tensor_tensor(
                                out=self.xs[:, tile, :], in0=self.xs[:, tile, :], in1=tmp, op=ALU.add),
                                reads=[b_tmp, self.xb[tile]], writes=[self.xb[tile]])
                        deferred.append(resid)


        while deferred:
            deferred.pop(0)()

    def mixer_jobs(self, l):
        i = l // 2
        jobs = []
        if l % 2 == 0:
            w = self.abin_d[i].rearrange("(kc p) f -> p kc f", p=128)
            sl = lambda c0: ((128, KC, 256), w[:, :, c0:c0 + 256])
            for grp in range(2):
                for p in range(2):
                    jobs.append(sl(grp * 512 + p * 256))
                    jobs.append(sl(2560 + grp * 512 + p * 256))
            jobs.append(sl(1024)); jobs.append(sl(1280))
            for p in range(2):
                jobs.append(sl(1536 + p * 256))
                jobs.append(sl(2048 + p * 256))
            wo = self.about_d[i].rearrange("(kc p) f -> p kc f", p=128)
        else:
            w = self.cdin_d[i].rearrange("(kc p) f -> p kc f", p=128)
            sl = lambda c0: ((128, KC, 256), w[:, :, c0:c0 + 256])
            for c0 in (0, 256, 512, 768, 1024, 1280, 1536, 1792, 2048, 2304):
                jobs.append(sl(c0))
            wo = self.cdout_d[i].rearrange("(kc p) f -> p kc f", p=128)
        for s4 in range(4):
            jobs.append(((128, KC, 256), wo[:, :, s4 * 256:(s4 + 1) * 256]))
        return jobs

    def mm_fm(self, bk, slab, b_slab, fc, hT, b_hT, tg):
        for kc in range(KC):
            self.kb.op("pe", lambda e, kc=kc: e.matmul(
                self.ps[:, bk, :], lhsT=slab[:, kc, fc * 128:(fc + 1) * 128],
                rhs=hT[:, kc, tg * 512:(tg + 1) * 512], start=(kc == 0), stop=(kc == KC - 1)),
                reads=[b_slab, b_hT], writes=[self.pb[bk]], inc=(kc == KC - 1))

    def mm_tm(self, bk, slabs, b_slabs, hT, b_hT, tile):
        for s2 in range(2):
            for kc in range(KC):
                self.kb.op("pe", lambda e, kc=kc, s2=s2: e.matmul(
                    self.ps[:, bk, s2 * 256:(s2 + 1) * 256], lhsT=hT[:, kc, tile * 128:(tile + 1) * 128],
                    rhs=slabs[s2][:, kc, :], start=(kc == 0), stop=(kc == KC - 1)),
                    reads=[b_slabs[s2], b_hT], writes=[self.pb[bk]], inc=(kc == KC - 1))

    def prenorm_all(self, l, gi, hT, b_hT, xn, b_xn, junk, b_junk):
        for tile in range(NT):
            self.prenorm_tile(l, gi, tile, xn, b_xn, junk, b_junk, hT[:, :, tile * 128:(tile + 1) * 128], b_hT)

    def out_proj(self, l, srcs, b_srcs, tmp, b_tmp, junk, b_junk):
        kb = self.kb
        self.load_gbc(l, 3)
        slabs = [self.ws.acquire((128, KC, 256)) for _ in range(4)]
        for tile in range(NT):
            b0 = self.next_bank(); b1 = self.next_bank()
            bb = [b0, b1]
            for s4 in range(4):
                for ch in range(8):
                    kb.op("pe", lambda e, s4=s4, ch=ch: e.matmul(
                        self.ps[:, bb[s4 // 2], (s4 % 2) * 256:(s4 % 2) * 256 + 256],
                        lhsT=srcs[ch][:, tile * 128:(tile + 1) * 128], rhs=slabs[s4][0][:, ch, :],
                        start=(ch == 0), stop=(ch == 7)),
                        reads=[slabs[s4][1], b_srcs[ch]], writes=[self.pb[bb[s4 // 2]]], inc=(ch == 7))
            for c in range(2):
                kb.op("act", lambda e, c=c: e.activation(out=junk[:, 0:512], in_=self.ps[:, bb[c], :],
                                                         func=AF.Square, accum_out=self.stat[:, 32 + c:33 + c]),
                      reads=[self.pb[bb[c]]], writes=[b_junk, self.b_stat])
            rs = self.stat[:, 34:35]
            kb.op("dve", lambda e: e.tensor_tensor(out=rs, in0=self.stat[:, 32:33], in1=self.stat[:, 33:34], op=ALU.add),
                  reads=[self.b_stat], writes=[self.b_stat])
            self.rsqrt(rs, rs, 1.0 / D, RMS_EPS)
            for c in range(2):
                kb.op("dve", lambda e, c=c: e.scalar_tensor_tensor(
                    out=tmp[:, c * 512:(c + 1) * 512], in0=self.ps[:, bb[c], :], scalar=rs,
                    in1=self.gbc[:, c * 512:(c + 1) * 512], op0=ALU.mult, op1=ALU.mult),
                    reads=[self.pb[bb[c]], self.b_stat, self.b_gbc], writes=[b_tmp])
            kb.op("dve", lambda e, tile=tile: e.tensor_tensor(
                out=self.xs[:, tile, :], in0=self.xs[:, tile, :], in1=tmp, op=ALU.add),
                reads=[b_tmp, self.xb[tile]], writes=[self.xb[tile]])

    def even_mixer(self, l):
        kb = self.kb
        i = l // 2
        kb.barrier()
        hT = self.av(0, [128, KC, S], BF16); b_hT = Buf("hT")
        qT = self.av(16384, [128, 4, S], BF16); b_qT = [Buf(f"qT{c}") for c in range(4)]
        kT = self.av(24576, [128, 4, S], BF16); b_kT = [Buf(f"kT{c}") for c in range(4)]
        vtok = self.av(32768, [128, NT, 512], BF16); b_v = Buf("vtok")
        cosT = self.av(40960, [128, S], F32)
        sinT = self.av(45056, [128, S], F32)
        b_cs = Buf("cs")
        t1 = self.av(49152, [128, 512], F32); t2 = self.av(50176, [128, 512], F32)
        b_t1, b_t2 = Buf("t1"), Buf("t2")
        xn = self.av(16384, [128, D], BF16); junk = self.av(16384 + D, [128, D], BF16)
        b_xn, b_junk = Buf("xn"), Buf("junk")
        kb.dma("sp", cosT, self.rope_d[0], self.sem_g, writes=[b_cs])
        kb.dma("sp", sinT, self.rope_d[1], self.sem_g, writes=[b_cs])
        self.prenorm_all(l, 2, hT, b_hT, xn, b_xn, junk, b_junk)
        kb.barrier()
        for grp in range(2):
            dst, b_dst = (qT, b_qT) if grp == 0 else (kT, b_kT)
            for p in range(2):
                wm, b_wm = self.ws.acquire((128, KC, 256))
                wp, b_wp = self.ws.acquire((128, KC, 256))
                for fc in range(2):
                    c = 2 * p + fc
                    for tg in range(4):
                        ba = self.next_bank(); bp = self.next_bank()
                        self.mm_fm(ba, wm, b_wm, fc, hT, b_hT, tg)
                        self.mm_fm(bp, wp, b_wp, fc, hT, b_hT, tg)
                        sl = slice(tg * 512, (tg + 1) * 512)
                        kb.op("dve", lambda e, ba=ba, sl=sl: e.tensor_tensor(out=t1, in0=self.ps[:, ba, :], in1=cosT[:, sl], op=ALU.mult),
                              reads=[self.pb[ba], b_cs], writes=[b_t1])
                        kb.op("dve", lambda e, bp=bp, sl=sl: e.tensor_tensor(out=t2, in0=self.ps[:, bp, :], in1=sinT[:, sl], op=ALU.mult),
                              reads=[self.pb[bp], b_cs], writes=[b_t2])
                        kb.op("dve", lambda e, c=c, sl=sl, dst=dst: e.tensor_tensor(out=dst[:, c, sl], in0=t1, in1=t2, op=ALU.add),
                              reads=[b_t1, b_t2], writes=[b_dst[c]])
        sv = [self.ws.acquire((128, KC, 256)) for _ in range(2)]
        for tile in range(NT):
            bk = self.next_bank()
            self.mm_tm(bk, [sv[0][0], sv[1][0]], [sv[0][1], sv[1][1]], hT, b_hT, tile)
            kb.op("act", lambda e, bk=bk, tile=tile: e.activation(out=vtok[:, tile, :], in_=self.ps[:, bk, :], func=AF.Copy),
                  reads=[self.pb[bk]], writes=[b_v])
        kb.barrier()
        HC = 30 + S
        hconv = self.av(40960, [128, 4, HC], BF16); b_hc = [Buf(f"hc{c}") for c in range(4)]
        sig = self.av(40960 + 4 * HC, [128, 512], F32); b_sig = Buf("sig")
        for c in range(4):
            kb.op("dve", lambda e, c=c: e.memset(hconv[:, c, 0:30], 0.0), writes=[b_hc[c]])
        for p in range(2):
            wa, b_wa = self.ws.acquire((128, KC, 256))
            wb, b_wb = self.ws.acquire((128, KC, 256))
            for fc in range(2):
                c = 2 * p + fc
                for tg in range(4):
                    ba = self.next_bank(); bg = self.next_bank()
                    self.mm_fm(ba, wa, b_wa, fc, hT, b_hT, tg)
                    self.mm_fm(bg, wb, b_wb, fc, hT, b_hT, tg)
                    kb.op("act", lambda e, bg=bg: e.activation(out=sig, in_=self.ps[:, bg, :], func=AF.Sigmoid),
                          reads=[self.pb[bg]], writes=[b_sig])
                    kb.op("dve", lambda e, ba=ba, c=c, tg=tg: e.tensor_tensor(
                        out=hconv[:, c, 30 + tg * 512:30 + (tg + 1) * 512], in0=self.ps[:, ba, :], in1=sig, op=ALU.mult),
                        reads=[self.pb[ba], b_sig], writes=[b_hc[c]])
        kb.barrier()
        oaT = self.av(0, [128, 4, S], BF16); b_oa = [Buf(f"oa{c}") for c in range(4)]
        mb_all = self.av(8192, [128, NT, 64], BF16); b_mb = Buf("mb")
        gate_sb = self.av(9216, [128, 64], F32); b_gate = Buf("gate")
        cmp = self.av(9344, [128, 512], F32); b_cmp = Buf("cmp")
        cnt = self.av(10368, [128, 64], F32); b_cnt = Buf("cnt")
        mbt = [self.arena[0:8, 10496 + k * 2048:10496 + (k + 1) * 2048] for k in range(2)]
        b_mbt = [Buf("mbt0"), Buf("mbt1")]
        pt = [self.av(14592 + k * 512, [128, 512], BF16) for k in range(2)] + \
             [self.av(9344 + k * 512, [128, 512], BF16) for k in range(2)]
        b_pt = [Buf(f"pt{k}") for k in range(4)]
        ptc = [0]
        rec = self.av(40960 + 4 * HC, [128, 512], F32); b_rec = Buf("rec")
        km32 = self.small[:, 0:32]; kmT = self.small[:, 32:48].bitcast(BF16)
        ident = self.cb[:, C_IDENT:C_IDENT + 128]
        for c in range(4):
            kb.op("dve", lambda e, c=c: e.tensor_reduce(
                out=km32[:, c * 8:(c + 1) * 8], in_=kT[:, c, :].rearrange("p (n t) -> p n t", t=256),
                axis=AX.X, op=ALU.add), reads=[b_kT[c]], writes=[self.b_small])
        kb.op("dve", lambda e: e.tensor_scalar(out=kmT, in0=km32, scalar1=1.0 / 256, scalar2=None, op0=ALU.mult),
              reads=[self.b_small], writes=[self.b_small])
        for tile in range(NT):
            own = tile // 2
            mbv = mb_all[:, tile, :].rearrange("p (h n) -> p h n", h=8)
            kb.op("dve", lambda e, tile=tile: e.memset(mb_all[:, tile, :], NEG_BIG), writes=[b_mb])
            kb.op("dve", lambda e, mbv=mbv, own=own: e.memset(mbv[:, :, 0:own + 1], 0.0), writes=[b_mb])
            if own <= 3:
                continue
            bk = self.next_bank()
            for h in range(8):
                c, pb_ = h // 2, 64 * (h % 2)
                kb.op("pe", lambda e, h=h, c=c, pb_=pb_: e.matmul(
                    self.ps[:, bk, h * 8:(h + 1) * 8], lhsT=qT[pb_:pb_ + 64, c, tile * 128:(tile + 1) * 128],
                    rhs=kmT[pb_:pb_ + 64, c * 8:(c + 1) * 8], start=True, stop=True),
                    reads=[b_qT[c], self.b_small], writes=[self.pb[bk]], inc=(h == 7))
            kb.op("act", lambda e, bk=bk: e.activation(out=gate_sb, in_=self.ps[:, bk, 0:64], func=AF.Copy),
                  reads=[self.pb[bk]], writes=[b_gate])
            g3 = gate_sb.rearrange("p (h n) -> p h n", h=8)[:, :, 0:own]
            cmpv = cmp[:, 0:8 * own * own].rearrange("p (h n m) -> p h n m", h=8, n=own)
            kb.op("dve", lambda e, g3=g3, cmpv=cmpv, own=own: e.tensor_tensor(
                out=cmpv, in0=g3.unsqueeze(2).to_broadcast([128, 8, own, own]),
                in1=g3.unsqueeze(3).to_broadcast([128, 8, own, own]), op=ALU.is_gt),
                reads=[b_gate], writes=[b_cmp])
            cntv = cnt[:, 0:8 * own].rearrange("p (h n) -> p h n", h=8)
            kb.op("dve", lambda e, cmpv=cmpv, cntv=cntv: e.tensor_reduce(out=cntv, in_=cmpv, axis=AX.X, op=ALU.add),
                  reads=[b_cmp], writes=[b_cnt])
            kb.op("dve", lambda e, cntv=cntv, mbv=mbv, own=own: e.tensor_scalar(
                out=mbv[:, :, 0:own], in0=cntv, scalar1=3.0, scalar2=NEG_BIG, op0=ALU.is_ge, op1=ALU.mult),
                reads=[b_cnt], writes=[b_mb])
        ones = self.cb[:, C_ONES:C_ONES + 128]
        tri = self.cb[:, C_TRI_LE:C_TRI_LE + 128]
        for h in range(8):
            c, pb_ = h // 2, 64 * (h % 2)
            mt, b_mt = mbt[h % 2], b_mbt[h % 2]
            for qg in range(2, 4):
                bk = self.next_bank()
                pst = self.bank(bk).bitcast(BF16)
                for k4 in range(4):
                    tile = 4 * qg + k4
                    kb.op("pe", lambda e, k4=k4, tile=tile: e.transpose(
                        out=pst[0:8, k4 * 128:(k4 + 1) * 128], in_=mb_all[:, tile, h * 8:(h + 1) * 8], identity=ident),
                        reads=[b_mb, self.b_const], writes=[self.pb[bk]], inc=(k4 == 3))
                kb.op("act", lambda e, qg=qg, pst=pst, mt=mt: e.activation(out=mt[:, qg * 512:(qg + 1) * 512], in_=pst[0:8, 0:512], func=AF.Copy),
                      reads=[self.pb[bk]], writes=[b_mt])
            for qg in range(4):
                bN = self.next_bank(hold=True); bD = self.next_bank(hold=True)
                kts = list(range(4 * qg + 4))
                info = {}

                def stage_s(kt):
                    q0 = max(qg * 512, kt * 128); off = q0 - qg * 512
                    bS = self.next_bank()
                    p_ = pt[ptc[0] % 4]; b_p = b_pt[ptc[0] % 4]; ptc[0] += 1
                    info[kt] = (off, q0, p_, b_p)
                    n = kt // 2
                    need_mask = (qg >= 2 and kt <= 4 * qg + 1)
                    kb.op("pe", lambda e: e.matmul(self.ps[:, bS, off:512], lhsT=kT[pb_:pb_ + 64, c, kt * 128:(kt + 1) * 128],
                                                   rhs=qT[pb_:pb_ + 64, c, q0:(qg + 1) * 512], start=True, stop=not need_mask),
                          reads=[b_kT[c], b_qT[c]], writes=[self.pb[bS]], inc=not need_mask)
                    if need_mask:
                        kb.op("pe", lambda e: e.matmul(self.ps[:, bS, off:512], lhsT=self.cb[0:8, C_OH + n * 128:C_OH + (n + 1) * 128],
                                                       rhs=mt[:, q0:(qg + 1) * 512], start=False, stop=True),
                              reads=[self.b_const, b_mt], writes=[self.pb[bS]])
                    kb.op("act", lambda e: e.activation(out=p_[:, off:512], in_=self.ps[:, bS, off:512], func=AF.Exp, scale=0.125),
                          reads=[self.pb[bS]], writes=[b_p])
                    if kt >= 4 * qg:
                        kb.op("dve", lambda e: e.tensor_tensor(out=p_[:, off:off + 128], in0=p_[:, off:off + 128], in1=tri, op=ALU.mult),
                              reads=[b_p, self.b_const], writes=[b_p])

                def stage_pv(kt):
                    off, q0, p_, b_p = info[kt]
                    last = (kt == kts[-1])
                    kb.op("pe", lambda e: e.matmul(self.ps[:, bN, off:512], lhsT=vtok[:, kt, c * 128:(c + 1) * 128],
                                                   rhs=p_[:, off:512], start=(kt == 0), stop=last),
                          reads=[b_v, b_p], writes=[self.pb[bN]], inc=last)
                    kb.op("pe", lambda e: e.matmul(self.ps[:, bD, off:512], lhsT=ones, rhs=p_[:, off:512],
                                                   start=(kt == 0), stop=last),
                          reads=[self.b_const, b_p], writes=[self.pb[bD]], inc=True)

                for idx in range(len(kts) + 2):
                    if idx < len(kts):
                        stage_s(kts[idx])
                    if idx >= 2:
                        stage_pv(kts[idx - 2])
                kb.op("dve", lambda e: e.reciprocal(out=rec[pb_:pb_ + 64, :], in_=self.ps[pb_:pb_ + 64, bD, :]),
                      reads=[self.pb[bD]], writes=[b_rec])
                kb.op("dve", lambda e: e.tensor_tensor(out=oaT[pb_:pb_ + 64, c, qg * 512:(qg + 1) * 512],
                                                       in0=self.ps[pb_:pb_ + 64, bN, :], in1=rec[pb_:pb_ + 64, :], op=ALU.mult),
                      reads=[self.pb[bN], b_rec], writes=[b_oa[c]])
                self.held.discard(bN); self.held.discard(bD)
        kb.barrier()
        acc = self.av(16384, [128, 4, S], F32); b_acc = [Buf(f"acc{c}") for c in range(4)]
        obT = self.av(32768, [128, 4, S], BF16); b_ob = [Buf(f"ob{c}") for c in range(4)]
        mean_t = self.av(8192, [128, 512], F32); rstd_t = self.av(9216, [128, 512], F32); yn = self.av(10240, [128, 512], F32)
        sq = [self.av(11264, [128, 512], F32), self.av(40960 + 4 * HC, [128, 512], F32)]
        b_mean, b_rstd, b_yn, b_sq = Buf("mean"), Buf("rstd"), Buf("yn"), [Buf("sq0"), Buf("sq1")]
        cw = self.small[:, 64:64 + CONV_FM]
        kb.dma("sp", cw, self.convfm_d[:, i * CONV_FM:(i + 1) * CONV_FM], self.sem_g, writes=[self.b_small])
        ones32 = self.small[:, 256:384]
        kb.op("dve", lambda e: e.memset(ones32, 1.0), writes=[self.b_small])
        dg = self.av(12288, [128, 31, 128], BF16); b_dg = Buf("dg")
        for c in range(4):
            for k in range(31):
                kb.op("dve", lambda e, c=c, k=k: e.tensor_scalar(out=dg[:, k, :], in0=ident, scalar1=cw[:, c * 31 + k:c * 31 + k + 1],
                                                                  scalar2=None, op0=ALU.mult),
                      reads=[self.b_const, self.b_small], writes=[b_dg])
            for tg in range(4):
                bk = self.next_bank()
                for k in range(31):
                    kb.op("pe", lambda e, c=c, k=k, tg=tg: e.matmul(
                        self.ps[:, bk, :], lhsT=dg[:, k, :], rhs=hconv[:, c, k + tg * 512:k + tg * 512 + 512],
                        start=(k == 0), stop=(k == 30)),
                        reads=[b_dg, b_hc[c]], writes=[self.pb[bk]], inc=(k == 30))
                kb.op("act", lambda e, c=c, tg=tg: e.activation(out=acc[:, c, tg * 512:(tg + 1) * 512], in_=self.ps[:, bk, :],
                                                                func=AF.Identity, bias=cw[:, 124 + c:125 + c], scale=1.0),
                      reads=[self.pb[bk], self.b_small], writes=[b_acc[c]])
        for tg in range(4):
            sl = slice(tg * 512, (tg + 1) * 512)
            b1 = self.next_bank(hold=True); b2 = self.next_bank(hold=True)
            for c in range(4):
                kb.op("pe", lambda e, c=c: e.matmul(self.ps[:, b1, :], lhsT=ones32, rhs=acc[:, c, sl], start=(c == 0), stop=(c == 3)),
                      reads=[self.b_small, b_acc[c]], writes=[self.pb[b1]], inc=(c == 3))
            for c in range(4):
                kb.op("act", lambda e, c=c: e.activation(out=sq[c % 2], in_=acc[:, c, sl], func=AF.Square),
                      reads=[b_acc[c]], writes=[b_sq[c % 2]])
                kb.op("pe", lambda e, c=c: e.matmul(self.ps[:, b2, :], lhsT=ones32, rhs=sq[c % 2], start=(c == 0), stop=(c == 3)),
                      reads=[self.b_small, b_sq[c % 2]], writes=[self.pb[b2]], inc=True)
            kb.op("dve", lambda e: e.tensor_scalar(out=mean_t, in0=self.ps[:, b1, :], scalar1=1.0 / 512, scalar2=None, op0=ALU.mult),
                  reads=[self.pb[b1]], writes=[b_mean])
            kb.op("dve", lambda e: e.tensor_tensor(out=yn, in0=mean_t, in1=mean_t, op=ALU.mult), reads=[b_mean], writes=[b_yn])
            kb.op("dve", lambda e: e.scalar_tensor_tensor(out=rstd_t, in0=self.ps[:, b2, :], scalar=1.0 / 512, in1=yn,
                                                          op0=ALU.mult, op1=ALU.subtract),
                  reads=[self.pb[b2], b_yn], writes=[b_rstd])
            kb.op("act", lambda e: e.activation(out=rstd_t, in_=rstd_t, func=AF.Sqrt, bias=self.eps_ap(LN_EPS), scale=1.0),
                  reads=[b_rstd, self.b_cf], writes=[b_rstd])
            kb.op("dve", lambda e: e.reciprocal(out=rstd_t, in_=rstd_t), reads=[b_rstd], writes=[b_rstd])
            self.held.discard(b1); self.held.discard(b2)
            for c in range(4):
                kb.op("dve", lambda e, c=c: e.tensor_tensor(out=yn, in0=acc[:, c, sl], in1=mean_t, op=ALU.subtract),
                      reads=[b_acc[c], b_mean], writes=[b_yn])
                kb.op("dve", lambda e: e.tensor_tensor(out=yn, in0=yn, in1=rstd_t, op=ALU.mult), reads=[b_yn, b_rstd], writes=[b_yn])
                kb.op("act", lambda e, c=c: e.activation(out=obT[:, c, sl], in_=yn, func=AF.Silu,
                                                         bias=cw[:, 132 + c:133 + c], scale=cw[:, 128 + c:129 + c]),
                      reads=[b_yn, self.b_small], writes=[b_ob[c]])
        tmp = self.av(8192, [128, D], F32); b_tmp = Buf("tmp")
        junk2 = self.av(12288, [128, 512], BF16); b_junk2 = Buf("junk2")
        kb.barrier()
        srcs = [oaT[:, c, :] for c in range(4)] + [obT[:, c, :] for c in range(4)]
        self.out_proj(l, srcs, b_oa + b_ob, tmp, b_tmp, junk2, b_junk2)

    def gelu(self, bk, dst, b_dst, ta, b_ta, tb, b_tb):
        kb = self.kb
        src = self.ps[:, bk, :]
        kb.op("act", lambda e: e.activation(out=ta, in_=src, func=AF.Square), reads=[self.pb[bk]], writes=[b_ta])
        kb.op("dve", lambda e: e.tensor_scalar(out=ta, in0=ta, scalar1=0.044715, scalar2=1.0, op0=ALU.mult, op1=ALU.add),
              reads=[b_ta], writes=[b_ta])
        kb.op("dve", lambda e: e.tensor_tensor(out=ta, in0=src, in1=ta, op=ALU.mult), reads=[b_ta, self.pb[bk]], writes=[b_ta])
        kb.op("act", lambda e: e.activation(out=tb, in_=ta, func=AF.Sigmoid, scale=1.5957691216057308),
              reads=[b_ta], writes=[b_tb])
        kb.op("dve", lambda e: e.tensor_tensor(out=dst, in0=src, in1=tb, op=ALU.mult), reads=[b_tb, self.pb[bk]], writes=[b_dst])

    def odd_mixer(self, l):
        kb = self.kb
        i = l // 2
        kb.barrier()
        hT = self.av(0, [128, KC, S], BF16); b_hT = Buf("hT")
        uT = self.av(16384, [128, 4, S], BF16); b_u = [Buf(f"u{c}") for c in range(4)]
        vln = self.av(24576, [128, NT, 512], BF16); b_vln = Buf("vln")
        ocT = self.av(32768, [128, 4, S], BF16); b_oc = [Buf(f"oc{c}") for c in range(4)]
        ta = self.av(40960, [128, 512], F32); tb = self.av(41984, [128, 512], F32); g32 = self.av(43008, [128, 512], F32)
        sgt = self.av(44032, [128, 512], F32)
        wmT = self.av(45056, [128, 4, 128], BF16)
        bias_bc = self.av(45568, [128, 512], F32)
        lng_bc = self.av(46592, [128, 512], F32); lnb_bc = self.av(47616, [128, 512], F32)
        w32 = self.av(48640, [128, 4, 128], F32)
        b_ta, b_tb, b_g32, b_sgt, b_wm, b_par = Buf("ta"), Buf("tb"), Buf("g32"), Buf("sgt"), Buf("wm"), Buf("par")
        xn = self.av(16384, [128, D], BF16); junk = self.av(16384 + D, [128, D], BF16)
        b_xn, b_junk = Buf("xn"), Buf("junk")
        kb.dma("sp", w32.rearrange("p a b -> p (a b)"), self.sguw_d[i], self.sem_g, writes=[b_par])
        kb.dma("sp", bias_bc, self.sgub_d[i:i + 1, :].to_broadcast([128, 512]), self.sem_g, writes=[b_par])
        kb.dma("sp", lng_bc, self.sguln_d[2 * i:2 * i + 1, :].to_broadcast([128, 512]), self.sem_g, writes=[b_par])
        kb.dma("sp", lnb_bc, self.sguln_d[2 * i + 1:2 * i + 2, :].to_broadcast([128, 512]), self.sem_g, writes=[b_par])
        tri_le = self.cb[:, C_TRI_LE:C_TRI_LE + 128]
        kb.op("dve", lambda e: e.tensor_tensor(out=wmT, in0=w32, in1=tri_le.unsqueeze(1).to_broadcast([128, 4, 128]), op=ALU.mult),
              reads=[b_par, self.b_const], writes=[b_wm])
        self.prenorm_all(l, 2, hT, b_hT, xn, b_xn, junk, b_junk)
        kb.barrier()
        for p in range(2):
            wu_, b_wu_ = self.ws.acquire((128, KC, 256))
            for fc in range(2):
                c = 2 * p + fc
                for tg in range(4):
                    bk = self.next_bank()
                    self.mm_fm(bk, wu_, b_wu_, fc, hT, b_hT, tg)
                    self.gelu(bk, uT[:, c, tg * 512:(tg + 1) * 512], b_u[c], ta, b_ta, tb, b_tb)
        sv = [self.ws.acquire((128, KC, 256)) for _ in range(2)]
        st = self.stat
        for tile in range(NT):
            bk = self.next_bank()
            self.mm_tm(bk, [sv[0][0], sv[1][0]], [sv[0][1], sv[1][1]], hT, b_hT, tile)
            self.gelu(bk, g32, b_g32, ta, b_ta, tb, b_tb)
            kb.op("dve", lambda e: e.tensor_reduce(out=st[:, 40:41], in_=g32, axis=AX.X, op=ALU.add),
                  reads=[b_g32], writes=[self.b_stat])
            kb.op("act", lambda e: e.activation(out=ta, in_=g32, func=AF.Square, accum_out=st[:, 41:42]),
                  reads=[b_g32], writes=[b_ta, self.b_stat])
            kb.op("dve", lambda e: e.tensor_scalar(out=st[:, 42:43], in0=st[:, 40:41], scalar1=1.0 / 512, scalar2=None, op0=ALU.mult),
                  reads=[self.b_stat], writes=[self.b_stat])
            kb.op("dve", lambda e: e.tensor_tensor(out=st[:, 43:44], in0=st[:, 42:43], in1=st[:, 42:43], op=ALU.mult),
                  reads=[self.b_stat], writes=[self.b_stat])
            kb.op("dve", lambda e: e.scalar_tensor_tensor(out=st[:, 44:45], in0=st[:, 41:42], scalar=1.0 / 512, in1=st[:, 43:44],
                                                          op0=ALU.mult, op1=ALU.subtract),
                  reads=[self.b_stat], writes=[self.b_stat])
            self.rsqrt(st[:, 44:45], st[:, 44:45], 1.0, LN_EPS)
            kb.op("dve", lambda e: e.tensor_scalar(out=g32, in0=g32, scalar1=st[:, 42:43], scalar2=st[:, 44:45],
                                                   op0=ALU.subtract, op1=ALU.mult),
                  reads=[b_g32, self.b_stat], writes=[b_g32])
            kb.op("dve", lambda e: e.tensor_tensor(out=g32, in0=g32, in1=lng_bc, op=ALU.mult), reads=[b_g32, b_par], writes=[b_g32])
            kb.op("dve", lambda e, tile=tile: e.tensor_tensor(out=vln[:, tile, :], in0=g32, in1=lnb_bc, op=ALU.add),
                  reads=[b_g32, b_par], writes=[b_vln])
        for g in range(4):
            for tg in range(4):
                bk = self.next_bank()
                for k4 in range(4):
                    kb.op("pe", lambda e, k4=k4: e.matmul(
                        self.ps[:, bk, k4 * 128:(k4 + 1) * 128], lhsT=vln[:, 4 * tg + k4, g * 128:(g + 1) * 128],
                        rhs=wmT[:, g, :], start=True, stop=True),
                        reads=[b_vln, b_wm], writes=[self.pb[bk]], inc=(k4 == 3))
                kb.op("dve", lambda e: e.tensor_tensor(
                    out=sgt.rearrange("p (a b) -> p a b", a=4), in0=self.ps[:, bk, :].rearrange("p (a b) -> p a b", a=4),
                    in1=bias_bc[:, g * 128:(g + 1) * 128].unsqueeze(1).to_broadcast([128, 4, 128]), op=ALU.add),
                    reads=[self.pb[bk], b_par], writes=[b_sgt])
                kb.op("dve", lambda e: e.tensor_tensor(out=ocT[:, g, tg * 512:(tg + 1) * 512], in0=sgt,
                                                       in1=uT[:, g, tg * 512:(tg + 1) * 512], op=ALU.mult),
                      reads=[b_sgt, b_u[g]], writes=[b_oc[g]])
        kb.barrier()
        qT = self.av(16384, [128, 4, S], BF16); b_qT = [Buf(f"qT{c}") for c in range(4)]
        kT = self.av(24576, [128, 4, S], BF16); b_kT = [Buf(f"kT{c}") for c in range(4)]
        vtok = self.av(40960, [128, NT, 512], BF16); b_v = Buf("vtok")
        for grp in range(2):
            dst, b_dst = (qT, b_qT) if grp == 0 else (kT, b_kT)
            for p in range(2):
                wm_, b_wm_ = self.ws.acquire((128, KC, 256))
                for fc in range(2):
                    c = 2 * p + fc
                    for tg in range(4):
                        bk = self.next_bank()
                        self.mm_fm(bk, wm_, b_wm_, fc, hT, b_hT, tg)
                        kb.op("act", lambda e, c=c, tg=tg, dst=dst: e.activation(
                            out=dst[:, c, tg * 512:(tg + 1) * 512], in_=self.ps[:, bk, :], func=AF.Copy,
                            scale=(1.0 if grp == 0 else 0.125)), reads=[self.pb[bk]], writes=[b_dst[c]])
        sv = [self.ws.acquire((128, KC, 256)) for _ in range(2)]
        for tile in range(NT):
            bk = self.next_bank()
            self.mm_tm(bk, [sv[0][0], sv[1][0]], [sv[0][1], sv[1][1]], hT, b_hT, tile)
            kb.op("act", lambda e, tile=tile: e.activation(out=vtok[:, tile, :], in_=self.ps[:, bk, :], func=AF.Copy),
                  reads=[self.pb[bk]], writes=[b_v])
        kb.barrier()
        odT = self.av(0, [128, 4, S], BF16); b_od = [Buf(f"od{c}") for c in range(4)]
        fb = [self.av(8192 + k * 1024, [128, 512], F32) for k in range(4)]
        spbf = [self.av(12288 + k * 512, [128, 512], BF16) for k in range(4)]
        att = [self.av(14336 + k * 512, [128, 512], BF16) for k in range(4)]
        b_fb, b_spbf, b_att = ([Buf(f"{n}{k}") for k in range(4)] for n in ("fb", "spbf", "att"))
        zeros = self.cb[:, C_ZEROS:C_ZEROS + 128]
        uneg = self.cb[:, C_UNEG:C_UNEG + 128]
        onesneg = self.cb[:, C_ONESNEG:C_ONESNEG + 128]
        tri_lt = self.cb[:, C_TRI_LT:C_TRI_LT + 128]
        one_ap = self.eps_ap(1.0)
        tiles = []
        for h in range(8):
            for qg in range(4):
                kts = list(range(4 * qg + 3, -1, -1))
                for kt in kts:
                    tiles.append((h, qg, kt, kt == kts[0], kt == 0))
        n_t = len(tiles)
        gst = {}
        zb = {}

        def geom(i):
            h, qg, kt, first, last = tiles[i]
            q0 = max(qg * 512, kt * 128)
            return h, qg, kt, first, last, h // 2, 64 * (h % 2), i % 4, q0, q0 - qg * 512

        def st_z(i):
            h, qg, kt, first, last, c, pb_, par, q0, off = geom(i)
            if first:
                bN = self.next_bank(hold=True); bR = self.next_bank(hold=True)
                gst[(h, qg)] = (bN, bR)
                for bz in (bN, bR):
                    kb.op("pe", lambda e, bz=bz: e.matmul(self.ps[:, bz, :], lhsT=zeros, rhs=qT[:, c, 0:512], start=True, stop=False),
                          reads=[self.b_const, b_qT[c]], writes=[self.pb[bz]], inc=True)
            bZ = self.next_bank(hold=True)
            zb[i] = bZ
            kb.op("pe", lambda e: e.matmul(self.ps[:, bZ, off:512], lhsT=kT[pb_:pb_ + 64, c, kt * 128:(kt + 1) * 128],
                                           rhs=qT[pb_:pb_ + 64, c, q0:(qg + 1) * 512], start=True, stop=True),
                  reads=[b_kT[c], b_qT[c]], writes=[self.pb[bZ]], inc=True)

        def st_sp(i):
            h, qg, kt, first, last, c, pb_, par, q0, off = geom(i)
            bZ = zb[i]
            kb.op("act", lambda e: e.activation(out=fb[par][:, off:512], in_=self.ps[:, bZ, off:512], func=AF.Exp),
                  reads=[self.pb[bZ]], writes=[b_fb[par]])
            kb.op("act", lambda e: e.activation(out=fb[par][:, off:512], in_=fb[par][:, off:512], func=AF.Ln, bias=one_ap, scale=1.0),
                  reads=[b_fb[par], self.b_cf], writes=[b_fb[par]])

        def st_cast(i):
            h, qg, kt, first, last, c, pb_, par, q0, off = geom(i)
            kb.op("dve", lambda e: e.tensor_copy(out=spbf[par][:, off:512], in_=fb[par][:, off:512]),
                  reads=[b_fb[par]], writes=[b_spbf[par]])
            if kt >= 4 * qg:
                kb.op("dve", lambda e: e.tensor_tensor(out=spbf[par][:, off:off + 128], in0=spbf[par][:, off:off + 128],
                                                       in1=tri_lt, op=ALU.mult),
                      reads=[b_spbf[par], self.b_const], writes=[b_spbf[par]])

        def st_y(i):
            h, qg, kt, first, last, c, pb_, par, q0, off = geom(i)
            bZ = zb[i]
            kb.op("pe", lambda e: e.matmul(self.ps[:, bZ, off:512], lhsT=uneg, rhs=spbf[par][:, off:512], start=False, stop=True),
                  reads=[self.b_const, b_spbf[par]], writes=[self.pb[bZ]], inc=True)

        def st_sub(i):
            h, qg, kt, first, last, c, pb_, par, q0, off = geom(i)
            bZ = zb[i]
            kb.op("dve", lambda e: e.tensor_tensor(out=fb[par][:, off:512], in0=self.ps[:, bZ, off:512],
                                                   in1=fb[par][:, off:512], op=ALU.subtract),
                  reads=[self.pb[bZ], b_fb[par]], writes=[b_fb[par]])
            self.held.discard(bZ)

        def st_add(i):
            h, qg, kt, first, last, c, pb_, par, q0, off = geom(i)
            bN, bR = gst[(h, qg)]
            kb.op("dve", lambda e: e.tensor_tensor(out=fb[par][:, off:512], in0=self.ps[:, bR, off:512],
                                                   in1=fb[par][:, off:512], op=ALU.add),
                  reads=[self.pb[bR], b_fb[par]], writes=[b_fb[par]])
            kb.op("pe", lambda e: e.matmul(self.ps[:, bR, off:512], lhsT=onesneg, rhs=spbf[par][:, off:512], start=False, stop=False),
                  reads=[self.b_const, b_spbf[par]], writes=[self.pb[bR]], inc=True)

        def st_att(i):
            h, qg, kt, first, last, c, pb_, par, q0, off = geom(i)
            kb.op("act", lambda e: e.activation(out=att[par][:, off:512], in_=fb[par][:, off:512], func=AF.Exp),
                  reads=[b_fb[par]], writes=[b_att[par]])
            if kt >= 4 * qg:
                kb.op("dve", lambda e: e.tensor_tensor(out=att[par][:, off:off + 128], in0=att[par][:, off:off + 128],
                                                       in1=tri_lt, op=ALU.mult),
                      reads=[b_att[par], self.b_const], writes=[b_att[par]])

        def st_pv(i):
            h, qg, kt, first, last, c, pb_, par, q0, off = geom(i)
            bN, bR = gst[(h, qg)]
            kb.op("pe", lambda e: e.matmul(self.ps[:, bN, off:512], lhsT=vtok[:, kt, c * 128:(c + 1) * 128],
                                           rhs=att[par][:, off:512], start=False, stop=last),
                  reads=[b_v, b_att[par]], writes=[self.pb[bN]], inc=True)
            if last:
                kb.op("dve", lambda e: e.tensor_copy(out=odT[pb_:pb_ + 64, c, qg * 512:(qg + 1) * 512], in_=self.ps[pb_:pb_ + 64, bN, :]),
                      reads=[self.pb[bN]], writes=[b_od[c]])
                self.held.discard(bN); self.held.discard(bR)

        for s_ in range(n_t + 4):
            if 0 <= s_ - 2 < n_t:
                st_y(s_ - 2)
            if s_ < n_t:
                st_z(s_)
            if 0 <= s_ - 1 < n_t:
                st_sp(s_ - 1)
            if 0 <= s_ - 3 < n_t:
                st_add(s_ - 3)
            if 0 <= s_ - 2 < n_t:
                st_sub(s_ - 2)
            if 0 <= s_ - 1 < n_t:
                st_cast(s_ - 1)
            if 0 <= s_ - 3 < n_t:
                st_att(s_ - 3)
            if 0 <= s_ - 4 < n_t:
                st_pv(s_ - 4)
        kb.barrier()
        tmp = self.av(8192, [128, D], F32); b_tmp = Buf("tmp")
        junk2 = self.av(12288, [128, 512], BF16); b_junk2 = Buf("junk2")
        srcs = [ocT[:, c, :] for c in range(4)] + [odT[:, c, :] for c in range(4)]
        self.out_proj(l, srcs, b_oc + b_od, tmp, b_tmp, junk2, b_junk2)


C_IDENT = 0
C_TRI_LE = 128
C_TRI_LT = 256
C_ONES = 384
C_UNEG = 512
C_ONESNEG = 640
C_ZEROS = 768
C_OH = 896
CB_COLS = 896 + 1024
CONV_FM = 4 * 31 + 12


def make_consts():
    cb = np.zeros((128, CB_COLS), np.float32)
    cb[:, C_IDENT:C_IDENT + 128] = np.eye(128, dtype=np.float32)
    j = np.arange(128)[:, None]
    t = np.arange(128)[None, :]
    cb[:, C_TRI_LE:C_TRI_LE + 128] = (j <= t)
    cb[:, C_TRI_LT:C_TRI_LT + 128] = (j < t)
    cb[:, C_ONES:C_ONES + 128] = 1.0
    cb[:, C_UNEG:C_UNEG + 128] = -(j > t).astype(np.float32)
    cb[:, C_ONESNEG:C_ONESNEG + 128] = -1.0
    for n in range(8):
        cb[n, C_OH + n * 128:C_OH + (n + 1) * 128] = 1.0
    return cb


def rope_tables():
    pos = np.arange(S, dtype=np.float32)
    inv = (np.float32(10000.0) ** (-np.arange(0, 64, 2, dtype=np.float32) / np.float32(64))).astype(np.float32)
    ang = (pos[:, None] * inv[None, :]).astype(np.float32)
    cos = np.cos(ang).astype(np.float32).T
    sin = np.sin(ang).astype(np.float32).T
    cs = np.zeros((2, 128, S), np.float32)
    for hh in range(2):
        cs[0, hh * 64:hh * 64 + 32] = cos
        cs[0, hh * 64 + 32:hh * 64 + 64] = cos
        cs[1, hh * 64:hh * 64 + 32] = -sin
        cs[1, hh * 64 + 32:hh * 64 + 64] = sin
    return cs


def host_inputs(inputs, b):
    ng = np.asarray(inputs["norm_g"], np.float32)
    g_fm = np.ascontiguousarray(ng.reshape(DEPTH * 6, KC, 128).transpose(2, 0, 1).reshape(128, DEPTH * 6 * KC))
    m = {
        "x": np.ascontiguousarray(inputs["x"][b]),
        "g_fm": g_fm,
        "norm_g": ng,
        "ffn_w_gate": inputs["ffn_w_gate"],
        "ffn_w_up": inputs["ffn_w_up"],
        "ffn_w_down": inputs["ffn_w_down"],
        "consts_bf": make_consts(),
    }
    w = np.asarray(inputs["ab_w_in"], np.float32)
    perm = np.concatenate([np.arange(h * 64, h * 64 + 64).reshape(2, 32)[::-1].reshape(-1) for h in range(8)])
    m["ab_w_in_ext"] = np.ascontiguousarray(
        np.concatenate([w, w[:, :, 0:512][:, :, perm], w[:, :, 512:1024][:, :, perm]], axis=2))
    m["ab_w_out"] = inputs["ab_w_out"]
    m["cd_w_in"] = inputs["cd_w_in"]
    m["cd_w_out"] = inputs["cd_w_out"]
    cf = np.zeros((128, 2 * CONV_FM), np.float32)
    for i in range(2):
        cw = np.asarray(inputs["conv_w"][i], np.float32)
        cf[:, i * CONV_FM:i * CONV_FM + 124] = cw.T.reshape(4, 128, 31).transpose(1, 0, 2).reshape(128, 124)
        for q, name in enumerate(("conv_b", "conv_ln_g", "conv_ln_b")):
            v = np.asarray(inputs[name][i], np.float32).reshape(4, 128).T
            cf[:, i * CONV_FM + 124 + 4 * q:i * CONV_FM + 128 + 4 * q] = v
    m["conv_fm"] = cf
    m["rope_cs"] = rope_tables()
    sw = np.asarray(inputs["sgu_w"], np.float32)
    m["sgu_wT"] = np.ascontiguousarray(sw.transpose(0, 3, 1, 2).reshape(2, 128, 512))
    m["sgu_b"] = np.ascontiguousarray(np.asarray(inputs["sgu_b"], np.float32).reshape(2, 512))
    m["sgu_ln"] = np.ascontiguousarray(np.stack([inputs["sgu_ln_g"][0], inputs["sgu_ln_b"][0],
                                                 inputs["sgu_ln_g"][1], inputs["sgu_ln_b"][1]]).astype(np.float32))
    return m


_PROG_CACHE = {}


def get_prog(layers, stages=None):
    key = (tuple(layers), None if stages is None else tuple(sorted(stages)))
    if key not in _PROG_CACHE:
        _PROG_CACHE[key] = Prog(layers, stages)
    return _PROG_CACHE[key]


def kernel(**inputs):
    inputs = {k: np.asarray(v) for k, v in inputs.items()}
    nb = inputs["x"].shape[0]
    prog = get_prog(range(DEPTH))
    in_maps = [host_inputs(inputs, b) for b in range(nb)]
    res = run_bass_kernel_spmd(prog.nc, in_maps, core_ids=list(range(nb)))
    return np.stack([np.asarray(r["out"]) for r in res.results], axis=0).astype(np.float32)
```
